# Optimizing a Trainium2 kernel written in Bass

```python
import math
import jax, jax.numpy as jnp
from jax import lax
import numpy as np

D_MODEL = 2048
BATCH = 2
SEQ = 4096
DEPTH = 1
DEC_BATCH = 8
DEC_SEQ = 64
PAST_LEN = 4096

CHUNK = 64
N_HEADS = 8
HEAD_DIM = 128
ATTN_W = N_HEADS * HEAD_DIM
CONV_GROUPS = 8
CONV_CH = D_MODEL - ATTN_W
CONV_WIDTH = 3
IN_COLS = 3 * ATTN_W + 3 * CONV_CH
D_FF = ((8 * D_MODEL + 3 * 256 - 1) // (3 * 256)) * 256
QBLOCK = 128
EPS = 1e-6

kernel_name = "stick_breaking_shortconv_hybrid_stream_step"


def rmsnorm(x, g):
    xf = x.astype(jnp.float32)
    y = xf * lax.rsqrt(jnp.mean(xf * xf, axis=-1, keepdims=True) + EPS) * g.astype(jnp.float32)
    return y.astype(x.dtype)


def stick_breaking(q, k, v, q_pos, k_pos):
    z = jnp.einsum('bqhd,bkhd->bhqk', q.astype(jnp.float32), k.astype(jnp.float32)) / math.sqrt(HEAD_DIM)
    mask = (k_pos[None, :] < q_pos[:, None])[None, None]
    log_1mb = jnp.where(mask, jax.nn.log_sigmoid(-z), 0.0)
    after = lax.cumsum(log_1mb, axis=3, reverse=True) - log_1mb
    w = jnp.where(mask, jnp.exp(jax.nn.log_sigmoid(z) + after), 0.0)
    return jnp.einsum('bhqk,bkhd->bqhd', w.astype(v.dtype), v)


def stick_breaking_blocks(q, k, v, q_offset):
    B, T, H, Dh = q.shape
    k_pos = jnp.arange(k.shape[1])
    if T <= QBLOCK:
        return stick_breaking(q, k, v, q_offset + jnp.arange(T), k_pos)
    nb = T // QBLOCK
    qb = q.reshape(B, nb, QBLOCK, H, Dh).transpose(1, 0, 2, 3, 4)
    starts = q_offset + jnp.arange(nb) * QBLOCK
    out = lax.map(lambda a: stick_breaking(a[0], k, v, a[1] + jnp.arange(QBLOCK), k_pos), (qb, starts))
    return out.transpose(1, 0, 2, 3, 4).reshape(B, T, H, Dh)


def hybrid_layer(x, k_past, v_past, conv_past, g_norm1, w_in, g_q, g_k, conv_w,
                 g_attn_out, g_conv_out, w_out, g_norm2, w_gate, w_up, w_down):
    B, T, _ = x.shape
    P = k_past.shape[1]
    hn = rmsnorm(x, g_norm1)
    proj = hn @ w_in
    q, k, v, gb, gc, hc = jnp.split(proj, [ATTN_W, 2 * ATTN_W, 3 * ATTN_W,
                                           3 * ATTN_W + CONV_CH, 3 * ATTN_W + 2 * CONV_CH], axis=-1)
    q = rmsnorm(q.reshape(B, T, N_HEADS, HEAD_DIM), g_q)
    k = rmsnorm(k.reshape(B, T, N_HEADS, HEAD_DIM), g_k)
    v = v.reshape(B, T, N_HEADS, HEAD_DIM)
    k_all = jnp.concatenate([k_past.astype(k.dtype), k], axis=1)
    v_all = jnp.concatenate([v_past.astype(v.dtype), v], axis=1)
    o_attn = stick_breaking_blocks(q, k_all, v_all, P).reshape(B, T, ATTN_W)
    u = gc * hc
    padded = jnp.concatenate([conv_past.astype(u.dtype), u], axis=1)
    conv = (conv_w[0] * padded[:, 0:T] + conv_w[1] * padded[:, 1:T + 1]
            + conv_w[2] * padded[:, 2:T + 2])
    o_conv = gb * conv
    new_conv = padded[:, -(CONV_WIDTH - 1):]
    mix = jnp.concatenate([rmsnorm(o_attn, g_attn_out), rmsnorm(o_conv, g_conv_out)], axis=-1)
    x = x + mix @ w_out
    h2 = rmsnorm(x, g_norm2)
    x = x + (jax.nn.silu(h2 @ w_gate) * (h2 @ w_up)) @ w_down
    return x, k, v, new_conv


def setup_inputs(seed: int = 0) -> dict:
    key = jax.random.key(seed)
    ks = jax.random.split(key, 20)
    f32 = jnp.float32
    nrm = lambda k, shape, s: jax.random.normal(k, shape, f32) * s
    return {
        "x_prompt": nrm(ks[0], (BATCH, SEQ, D_MODEL), 1.0),
        "x_sample": nrm(ks[1], (DEC_BATCH, DEC_SEQ, D_MODEL), 1.0),
        "cache_k": nrm(ks[2], (DEPTH, DEC_BATCH, PAST_LEN, N_HEADS, HEAD_DIM), 1.0),
        "cache_v": nrm(ks[3], (DEPTH, DEC_BATCH, PAST_LEN, N_HEADS, HEAD_DIM), 1.0),
        "state_conv": nrm(ks[4], (DEPTH, DEC_BATCH, CONV_WIDTH - 1, CONV_CH), 1.0),
        "g_norm1": 1.0 + nrm(ks[5], (DEPTH, D_MODEL), 0.01),
        "w_in": nrm(ks[6], (DEPTH, D_MODEL, IN_COLS), D_MODEL ** -0.5),
        "g_q": 1.0 + nrm(ks[7], (DEPTH, HEAD_DIM), 0.01),
        "g_k": 1.0 + nrm(ks[8], (DEPTH, HEAD_DIM), 0.01),
        "conv_w": nrm(ks[9], (DEPTH, CONV_WIDTH, CONV_CH), CONV_WIDTH ** -0.5),
        "g_attn_out": 1.0 + nrm(ks[10], (DEPTH, ATTN_W), 0.01),
        "g_conv_out": 1.0 + nrm(ks[11], (DEPTH, CONV_CH), 0.01),
        "w_out": nrm(ks[12], (DEPTH, D_MODEL, D_MODEL), D_MODEL ** -0.5),
        "g_norm2": 1.0 + nrm(ks[13], (DEPTH, D_MODEL), 0.01),
        "w_gate": nrm(ks[14], (DEPTH, D_MODEL, D_FF), D_MODEL ** -0.5),
        "w_up": nrm(ks[15], (DEPTH, D_MODEL, D_FF), D_MODEL ** -0.5),
        "w_down": nrm(ks[16], (DEPTH, D_FF, D_MODEL), D_FF ** -0.5),
    }


def reference(x_prompt, x_sample, cache_k, cache_v, state_conv, g_norm1, w_in, g_q, g_k,
              conv_w, g_attn_out, g_conv_out, w_out, g_norm2, w_gate, w_up, w_down):
    yp, ys = x_prompt, x_sample
    kp_l, vp_l, cp_l, ks_l, vs_l, cs_l = [], [], [], [], [], []
    for l in range(DEPTH):
        wts = (g_norm1[l], w_in[l], g_q[l], g_k[l], conv_w[l], g_attn_out[l], g_conv_out[l],
               w_out[l], g_norm2[l], w_gate[l], w_up[l], w_down[l])
        kp0 = jnp.zeros((yp.shape[0], 0, N_HEADS, HEAD_DIM), yp.dtype)
        cp0 = jnp.zeros((yp.shape[0], CONV_WIDTH - 1, CONV_CH), yp.dtype)
        yp, kp, vp, cp = hybrid_layer(yp, kp0, kp0, cp0, *wts)
        ys, kn, vn, cn = hybrid_layer(ys, cache_k[l], cache_v[l], state_conv[l], *wts)
        kp_l.append(kp); vp_l.append(vp); cp_l.append(cp)
        ks_l.append(kn); vs_l.append(vn); cs_l.append(cn)
    k_prompt = jnp.stack(kp_l); v_prompt = jnp.stack(vp_l); conv_prompt = jnp.stack(cp_l)
    k_sample = jnp.stack(ks_l); v_sample = jnp.stack(vs_l); conv_sample = jnp.stack(cs_l)
    return (yp, ys, k_prompt, v_prompt, conv_prompt, k_sample, v_sample, conv_sample)
```

```python
import contextlib
import os as _os
import numpy as np
import ml_dtypes
import concourse.bass as bass
import concourse.mybir as mybir
from concourse.bass_utils import run_bass_kernel_spmd

F32 = mybir.dt.float32
BF = mybir.dt.bfloat16
AF = mybir.ActivationFunctionType
ALU = mybir.AluOpType
EPS = 1e-6
NEG = -30000.0
D = 2048
DFF = 5632
NOWN = 1088
CENG = ('pe', 'act', 'dve', 'pool')
SEG = 1000


class Sched:
    def __init__(self, nc, sem_pool):
        self.nc = nc
        self.sem_pool = list(sem_pool)
        self.engsem = {e: [] for e in CENG}
        self.dmasem = {}
        self.dmacount = {}
        self.sigcount = {e: 0 for e in CENG}
        self.waited = {}
        self.ops = []

    def op(self, eng, fn, r=(), w=(), dma=None):
        self.ops.append(dict(eng=eng, fn=fn, r=tuple(r), w=tuple(w), dma=dma, sig=False, waits=[]))

    def flush(self, block):
        ops = self.ops
        self.ops = []
        lastw, readers = {}, {}
        dcount = dict(self.dmacount)
        for i, o in enumerate(ops):
            deps = {}
            for k in o['r']:
                if k in lastw:
                    deps[lastw[k]] = True
                if k[0] == 'p' and k[1:2].isupper():
                    for rd in readers.get(k, ()):
                        if ops[rd]['eng'] != o['eng']:
                            deps.setdefault(rd, False)
            for k in o['w']:
                if k in lastw:
                    deps.setdefault(lastw[k], False)
                for rd in readers.get(k, ()):
                    deps.setdefault(rd, False)
            deps.pop(i, None)
            for p, raw in deps.items():
                po = ops[p]
                if po['dma'] is not None:
                    o['waits'].append(('dma', po['dma'], 16 * dcount[po['dma']]))
                    continue
                same = po['eng'] == o['eng']
                if same and o['dma'] is None and o['eng'] == 'pe':
                    continue
                po['sig'] = True
                o['waits'].append(('eng', po['eng'], p))
            if o['dma'] is not None:
                if o['dma'] not in self.dmasem:
                    self.dmasem[o['dma']] = self.sem_pool.pop(0)
                dcount[o['dma']] = dcount.get(o['dma'], 0) + 1
            for k in o['r']:
                readers.setdefault(k, []).append(i)
            for k in o['w']:
                lastw[k] = i
                readers[k] = []
        for o in ops:
            if o['sig']:
                c = self.sigcount[o['eng']]
                self.sigcount[o['eng']] = c + 1
                if c // SEG >= len(self.engsem[o['eng']]):
                    self.engsem[o['eng']].append(self.sem_pool.pop())
                o['sigsem'] = self.engsem[o['eng']][c // SEG]
                o['sigval'] = c % SEG + 1
        self.dmacount = dcount
        engmap = {'pe': block.tensor, 'act': block.scalar, 'dve': block.vector,
                  'pool': block.gpsimd, 'sp': block.sync}
        for ename, deco in engmap.items():
            mine = [o for o in ops if o['eng'] == ename]
            last = (ename == 'sp')
            if not mine and not last:
                continue

            def body(eng, mine=mine, ename=ename, last=last):
                for o in mine:
                    need = {}
                    for wt in o['waits']:
                        if wt[0] == 'dma':
                            sem, val = self.dmasem[wt[1]], wt[2]
                        else:
                            sem, val = ops[wt[2]]['sigsem'], ops[wt[2]]['sigval']
                        sid = id(sem)
                        if val > need.get(sid, (None, 0))[1]:
                            need[sid] = (sem, val)
                    for sid, (sem, val) in need.items():
                        if self.waited.get((ename, sid), 0) >= val:
                            continue
                        self.waited[(ename, sid)] = val
                        eng.wait_ge(sem, val)
                    ins = o['fn'](eng)
                    if o['sig']:
                        ins.then_inc(o['sigsem'], 1)
                    if o['dma'] is not None:
                        ins.then_inc(self.dmasem[o['dma']], 16)
                if last:
                    for key, sem in self.dmasem.items():
                        val = 16 * self.dmacount[key]
                        if val and self.waited.get((ename, id(sem)), 0) < val:
                            self.waited[(ename, id(sem))] = val
                            eng.wait_ge(sem, val)
            deco(body)


def build(stop_after=None, skip=()):
    nc = bass.Bass("TRN2", target_bir_lowering=False)
    d = {}

    def din(name, shape, dt=F32):
        d[name] = nc.dram_tensor(name, list(shape), dt, kind="ExternalInput").ap()

    def dout(name, shape, dt=F32):
        d[name] = nc.dram_tensor(name, list(shape), dt, kind="ExternalOutput").ap()

    din("x_ctx", [4096, D]); din("x_smp", [64, D]); din("ck", [4096, 1024]); din("cv", [4096, 1024])
    din("w_in", [D, 6144]); din("w_out", [D, D]); din("w_gate", [D, DFF]); din("w_up", [D, DFF])
    din("w_down", [DFF, D])
    din("g1bc", [128, D]); din("g2bc", [128, D]); din("gqbc", [128, 128]); din("gkbc", [128, 128])
    din("gaT", [128, 8]); din("gcT", [128, 8]); din("cwT", [128, 24]); din("scT", [128, 16])
    din("ident", [128, 128], BF); din("negT", [128, 128], BF); din("negones", [128, 128], BF)
    din("mask3", [128, 512], BF); din("masktri", [128, 128], BF); din("masks4", [128, 256], BF)
    din("ones32", [128, 1]); din("onesb", [128, 1], BF); din("onesm", [128, 128], BF)
    dout("y_p", [1024, D]); dout("y_s", [64, D]); dout("k_p", [1024, 1024]); dout("v_p", [1024, 1024])
    dout("k_s", [64, 1024]); dout("v_s", [64, 1024]); dout("conv_pT", [128, 16]); dout("conv_sT", [128, 16])

    if stop_after == 'C2':
        dout("dbgA", [128, 8 * NOWN], BF); dout("dbgC", [128, 8 * NOWN], BF); dout("dbgS", [128, 18])
    w_in_v = d["w_in"].rearrange("(kc p) n -> p kc n", p=128)

    with contextlib.ExitStack() as es:
        E = es.enter_context
        sems = [E(nc.semaphore(f"sm{i}")) for i in range(96)]
        S = Sched(nc, sems)

        uid = [0]

        def sb(name, shape, dt, stack=None):
            uid[0] += 1
            return (stack or es).enter_context(nc.sbuf_tensor(f"{name}_{uid[0]}", list(shape), dt))

        def ps(name, shape, dt, stack):
            uid[0] += 1
            return stack.enter_context(nc.psum_tensor(f"{name}_{uid[0]}", list(shape), dt))

        def mm(out, lhsT, rhs, start, stop, r=(), w=()):
            S.op('pe', lambda e: e.matmul(out, lhsT=lhsT, rhs=rhs, start=start, stop=stop), r=r, w=w)

        def tr(out, in_, r=(), w=()):
            S.op('pe', lambda e: e.transpose(out=out, in_=in_, identity=identb[:in_.shape[0], :in_.shape[0]]), r=r, w=w)

        def act(out, in_, func, r=(), w=(), **kw):
            S.op('act', lambda e: e.activation(out=out, in_=in_, func=func, **kw), r=r, w=w)

        def stt(out, in0, scalar, in1, op0, op1, r=(), w=()):
            S.op('dve', lambda e: e.scalar_tensor_tensor(out=out, in0=in0, scalar=scalar, in1=in1, op0=op0, op1=op1), r=r, w=w)

        def tt(eng, out, in0, in1, op, r=(), w=()):
            S.op(eng, lambda e: e.tensor_tensor(out=out, in0=in0, in1=in1, op=op), r=r, w=w)

        def ts(eng, out, in0, s1, op0, r=(), w=()):
            S.op(eng, lambda e: e.tensor_scalar(out=out, in0=in0, scalar1=s1, scalar2=None, op0=op0), r=r, w=w)

        def cp(eng, out, in_, r=(), w=()):
            if eng == 'act':
                act(out, in_, AF.Copy, r=r, w=w)
            else:
                S.op(eng, lambda e: e.tensor_copy(out=out, in_=in_), r=r, w=w)

        def dma(eng, out, in_, key, r=(), w=()):
            S.op(eng, lambda e: e.dma_start(out=out, in_=in_), r=r, w=w, dma=key)

        def rstd_ops(ss, lnv, rs, scale, keys, bias=EPS):
            act(lnv, ss, AF.Ln, r=[keys[0]], w=[keys[1]], scale=scale, bias=bias)
            act(rs, lnv, AF.Exp, r=[keys[1]], w=[keys[2]], scale=-0.5)

        identb = sb("identb", [128, 128], BF); negTb = sb("negTb", [128, 128], BF)
        negonesb = sb("negonesb", [128, 128], BF); mask3b = sb("mask3b", [128, 512], BF)
        masktrib = sb("masktrib", [128, 128], BF); masks4b = sb("masks4b", [128, 256], BF)
        ones32 = sb("ones32s", [128, 1], F32); onesb = sb("onesbs", [128, 1], BF); onesm = sb("onesms", [128, 128], BF)
        gaT = sb("gaTs", [128, 8], F32); gcT = sb("gcTs", [128, 8], F32)
        cwT = sb("cwTs", [128, 24], F32); scT = sb("scTs", [128, 16], F32)
        gqs = sb("gqs", [128, 128], F32); gkb = sb("gkb", [128, 128], F32)
        ssq_a = sb("ssq_a", [128, 9], F32); ssq_c = sb("ssq_c", [128, 9], F32)
        ra = sb("ra", [128, 9], F32); rc = sb("rc", [128, 9], F32)
        lna = sb("lna", [128, 9], F32); lnc = sb("lnc", [128, 9], F32)
        mixA = sb("mixA", [128, 8, NOWN], BF)
        knew = sb("knew", [128, 8, 128], BF)
        vnew = sb("vnew", [128, 1024], BF)

        own_of_ctx = {12 + i: i for i in range(4)}
        own_of_ctx.update({28 + i: 4 + i for i in range(4)})

        with nc.Block() as blk:
            for nm, t in (("ident", identb), ("negT", negTb), ("negones", negonesb), ("mask3", mask3b),
                          ("masktri", masktrib), ("masks4", masks4b), ("ones32", ones32), ("onesb", onesb), ("onesm", onesm), ("gaT", gaT),
                          ("gcT", gcT), ("cwT", cwT), ("scT", scT), ("gqbc", gqs), ("gkbc", gkb)):
                dma('sp', t[:], d[nm][:, :], 'c0', w=[nm])
            ts('dve', gqs[:], gqs[:], float(128.0 ** -0.5), ALU.mult, r=["gqbc"], w=["gqs"])
            S.op('pool', lambda e: e.memset(knew[:], 0.0), w=["knew"])
            S.op('pool', lambda e: e.memset(vnew[:], 0.0), w=["vnew"])
            S.flush(blk)

        with contextlib.ExitStack() as esAB:
            kT = sb("kTc", [128, 4, 4096], BF, esAB)
            Vc = sb("Vc", [128, 32, 512], BF, esAB)
            qT = sb("qTc", [128, 4, NOWN], BF, esAB)
            for hg in range(2):
                with contextlib.ExitStack() as esA, nc.Block() as blk:
                    wq = sb("wq", [128, 16, 512], BF, esA); wk = sb("wk", [128, 16, 512], BF, esA)
                    wv = sb("wv", [128, 16, 512], BF, esA)
                    g1b = sb("g1b", [128, D], F32, esA)
                    xb = [sb(f"xb{i}", [128, D], F32, esA) for i in range(2)]
                    hn = [sb(f"hn{i}", [128, D], BF, esA) for i in range(2)]
                    hnT = [sb(f"hnT{i}", [128, 16, 128], BF, esA) for i in range(2)]
                    ss = [sb(f"ss{i}", [128, 1], F32, esA) for i in range(2)]
                    lnv = [sb(f"lnv{i}", [128, 1], F32, esA) for i in range(2)]
                    rs = [sb(f"rs{i}", [128, 1], F32, esA) for i in range(2)]
                    ssk = [sb(f"ssk{i}", [128, 8], F32, esA) for i in range(2)]
                    lnk = [sb(f"lnk{i}", [128, 8], F32, esA) for i in range(2)]
                    rk = [sb(f"rk{i}", [128, 8], F32, esA) for i in range(2)]
                    k32 = [sb(f"k32{i}", [128, 512], F32, esA) for i in range(2)]
                    v32 = [sb(f"v32{i}", [128, 512], F32, esA) for i in range(2)]
                    kbf = [sb(f"kbf{i}", [128, 512], BF, esA) for i in range(2)]
                    q32 = sb("q32", [128, 512], F32, esA); qbf = sb("qbf", [128, 512], BF, esA)
                    junk = sb("junk", [128, 512], BF, esA)
                    pT = ps("pT", [128, 16, 128], BF, esA)
                    pKV = [ps(f"pKV{i}", [128, 512], F32, esA) for i in range(4)]
                    pKT = ps("pKT", [128, 8, 128], BF, esA)

                    dma('sp', g1b[:], d["g1bc"][:, :], 'c0', w=["g1b"])
                    for i, wt, c0 in ((1, wk, 1024), (2, wv, 2048), (0, wq, 0)):
                        for half in range(2):
                            dma('pool', wt[:, half * 8:(half + 1) * 8, :],
                                w_in_v[:, half * 8:(half + 1) * 8, c0 + hg * 512: c0 + hg * 512 + 512],
                                f'wA{i}', w=[f"w{i}"])
                    blocks = [(t, 128, d["x_ctx"][t * 128:(t + 1) * 128, :], own_of_ctx.get(t)) for t in range(32)]
                    blocks.append((32, 64, d["x_smp"][:, :], 8))

                    def load_x(bi):
                        t, R, src, om = blocks[bi]
                        dma('sp', xb[bi % 2][:R, :], src, f"xb{bi % 2}", w=[f"xb{bi % 2}"])

                    rotc = [0]

                    def FE(bi):
                        t, R, src, om = blocks[bi]
                        i2 = bi % 2
                        X, H, HT = f"xb{i2}", f"hn{i2}", f"hnT{i2}"
                        act(hn[i2][:R, :], xb[i2][:R, :], AF.Square, r=[X], w=[H, f"ss{i2}"], accum_out=ss[i2][:R, :])
                        rstd_ops(ss[i2][:R, :], lnv[i2][:R, :], rs[i2][:R, :], 1.0 / D, [f"ss{i2}", f"lnv{i2}", f"rs{i2}"])
                        stt(hn[i2][:R, :], xb[i2][:R, :], rs[i2][:R, :], g1b[:R, :], ALU.mult, ALU.mult,
                            r=[X, f"rs{i2}", "g1b"], w=[H])

                    def FE2(bi):
                        t, R, src, om = blocks[bi]
                        i2 = bi % 2
                        X, H, HT = f"xb{i2}", f"hn{i2}", f"hnT{i2}"
                        for kc in range(8):
                            tr(pT[:, kc, :R], hn[i2][:R, kc * 128:(kc + 1) * 128], r=[H, "ident"], w=["pTa"])
                        cp('dve', hnT[i2][:, 0:8, :R], pT[:, 0:8, :R], r=["pTa"], w=[HT + "a"])
                        for kc in range(8, 16):
                            tr(pT[:, kc, :R], hn[i2][:R, kc * 128:(kc + 1) * 128], r=[H, "ident"], w=["pTb"])
                        cp('act', hnT[i2][:, 8:16, :R], pT[:, 8:16, :R], r=["pTb"], w=[HT + "b"])

                    def BE(bi, mid=None):
                        t, R, src, om = blocks[bi]
                        i2 = bi % 2
                        X, H, HT = f"xb{i2}", f"hn{i2}", f"hnT{i2}"
                        rot = rotc[0]
                        own = om is not None
                        pk = rot % 4; rot += 1
                        for kc in range(16):
                            mm(pKV[pk][:R, :], hnT[i2][:, kc, :R], wk[:, kc, :], kc == 0, kc == 15, r=[HT + ("a" if kc < 8 else "b"), "w1"], w=[f"pKV{pk}"])
                        for h in range(4):
                            act(junk[:R, h * 128:(h + 1) * 128], pKV[pk][:R, h * 128:(h + 1) * 128], AF.Square,
                                r=[f"pKV{pk}"], w=["junk", f"ssk{i2}"], accum_out=ssk[i2][:R, h:h + 1])
                        rstd_ops(ssk[i2][:R, :4], lnk[i2][:R, :4], rk[i2][:R, :4], 1.0 / 128, [f"ssk{i2}", f"lnk{i2}", f"rk{i2}"])
                        kdst = k32[i2] if own else kbf[i2]
                        kkey = f"k32{i2}" if own else f"kbf{i2}"
                        for h in range(4):
                            stt(kdst[:R, h * 128:(h + 1) * 128], pKV[pk][:R, h * 128:(h + 1) * 128], rk[i2][:R, h:h + 1],
                                gkb[:R, :], ALU.mult, ALU.mult, r=[f"pKV{pk}", f"rk{i2}"], w=[kkey])
                        if own:
                            cp('dve', kbf[i2][:R, :], k32[i2][:R, :], r=[kkey], w=[f"kbf{i2}"])
                            if om < 8:
                                dma('sp', d["k_p"][om * 128:(om + 1) * 128, hg * 512:(hg + 1) * 512], k32[i2][:, :], f"ko{i2}", r=[kkey])
                            else:
                                dma('sp', d["k_s"][:, hg * 512:(hg + 1) * 512], k32[i2][:R, :], f"ko{i2}", r=[kkey])
                        pv = rot % 4; rot += 1
                        for kc in range(16):
                            mm(pKV[pv][:R, :], hnT[i2][:, kc, :R], wv[:, kc, :], kc == 0, kc == 15, r=[HT + ("a" if kc < 8 else "b"), "w2"], w=[f"pKV{pv}"])
                        if own:
                            cp('act', v32[i2][:R, :], pKV[pv][:R, :], r=[f"pKV{pv}"], w=[f"v32{i2}"])
                            if om < 8:
                                cp('dve', Vc[:, t, :], v32[i2][:, :], r=[f"v32{i2}"], w=[f"V{t}"])
                                dma('sp', d["v_p"][om * 128:(om + 1) * 128, hg * 512:(hg + 1) * 512], v32[i2][:, :], f"vo{i2}", r=[f"v32{i2}"])
                            else:
                                cp('dve', vnew[:R, hg * 512:(hg + 1) * 512], v32[i2][:R, :], r=[f"v32{i2}"], w=["vnew"])
                                dma('sp', d["v_s"][:, hg * 512:(hg + 1) * 512], v32[i2][:R, :], f"vo{i2}", r=[f"v32{i2}"])
                        else:
                            cp('act', Vc[:, t, :], pKV[pv][:, :], r=[f"pKV{pv}"], w=[f"V{t}"])
                        if own:
                            pq = rot % 4; rot += 1
                            for kc in range(16):
                                mm(pKV[pq][:R, :], hnT[i2][:, kc, :R], wq[:, kc, :], kc == 0, kc == 15, r=[HT + ("a" if kc < 8 else "b"), "w0"], w=[f"pKV{pq}"])
                            for h in range(4):
                                act(junk[:R, h * 128:(h + 1) * 128], pKV[pq][:R, h * 128:(h + 1) * 128], AF.Square,
                                    r=[f"pKV{pq}"], w=["junk", f"ssq{i2}"], accum_out=ssk[i2][:R, 4 + h:5 + h])
                            rstd_ops(ssk[i2][:R, 4:8], lnk[i2][:R, 4:8], rk[i2][:R, 4:8], 1.0 / 128, [f"ssq{i2}", f"lnq{i2}", f"rq{i2}"])
                            for h in range(4):
                                stt(q32[:R, h * 128:(h + 1) * 128], pKV[pq][:R, h * 128:(h + 1) * 128], rk[i2][:R, 4 + h:5 + h],
                                    gqs[:R, :], ALU.mult, ALU.mult, r=[f"pKV{pq}", f"rq{i2}"], w=["q32"])
                            cp('dve', qbf[:R, :], q32[:R, :], r=["q32"], w=["qbf"])
                        if mid is not None:
                            mid()
                        for h in range(4):
                            tr(pKT[:, h, :R], kbf[i2][:R, h * 128:(h + 1) * 128], r=[f"kbf{i2}", "ident"], w=["pKT"])
                        if t < 32:
                            cp('act', kT[:, :, t * 128:(t + 1) * 128], pKT[:, 0:4, :], r=["pKT"], w=[f"kT{t}"])
                        else:
                            cp('act', knew[:, hg * 4:(hg + 1) * 4, :R], pKT[:, 0:4, :R], r=["pKT"], w=["knew"])
                        if own:
                            for h in range(4):
                                tr(pKT[:, 4 + h, :R], qbf[:R, h * 128:(h + 1) * 128], r=["qbf", "ident"], w=["pKT"])
                            cp('act', qT[:, :, om * 128: om * 128 + R], pKT[:, 4:8, :R], r=["pKT"], w=[f"qT{om}"])
                        rotc[0] = rot

                    load_x(0)
                    load_x(1)
                    FE(0)
                    FE2(0)
                    for bi in range(len(blocks)):
                        if bi + 2 < len(blocks):
                            load_x(bi + 2)
                        if bi + 1 < len(blocks):
                            FE(bi + 1)
                            BE(bi, mid=lambda b=bi + 1: FE2(b))
                        else:
                            BE(bi)
                    if 'A' in skip:
                        S.ops = []
                    S.flush(blk)
                if stop_after == 'A' and hg == 0:
                    return nc

                with contextlib.ExitStack() as esB:
                    kTs = sb("kTs", [128, 4, 4096], BF, esB)
                    Vs = sb("Vs", [128, 32, 512], BF, esB)
                    with contextlib.ExitStack() as esB1, nc.Block() as blk:
                        e32 = [sb(f"e32{i}", [128, 512], F32, esB1) for i in range(2)]
                        spb = [sb(f"spb{i}", [128, 512], BF, esB1) for i in range(2)]
                        acc32 = [sb(f"acc32{i}", [128, 512], F32, esB1) for i in range(2)]
                        accbf = [sb(f"accbf{i}", [128, 512], BF, esB1) for i in range(3)]
                        wtb = [sb(f"wtb{i}", [128, 512], BF, esB1) for i in range(2)]
                        osq = sb("osq", [128, 512], F32, esB1)
                        osqh = sb("osqh", [128, 512], BF, esB1); osql = sb("osql", [128, 512], BF, esB1)
                        pS = [ps(f"pS{i}", [128, 512], F32, esB1) for i in range(2)]
                        pW = [ps(f"pW{i}", [128, 512], F32, esB1) for i in range(2)]
                        pO = [ps(f"pO{i}", [128, 512], F32, esB1) for i in range(2)]
                        pM = ps("pM", [128, 512], F32, esB1)
                        kst = [sb(f"kst{i}", [128, 4, 512], BF, esB1) for i in range(2)]
                        pT2 = ps("pT2", [128, 2, 512], BF, esB1)
                        ckv = d["ck"].rearrange("(j p) n -> p j n", p=128)
                        cvv = d["cv"].rearrange("(j p) n -> p j n", p=128)
                        for q4 in range(4):
                            dma('pool', Vs[:, q4 * 8:(q4 + 1) * 8, :], cvv[:, q4 * 8:(q4 + 1) * 8, hg * 512:(hg + 1) * 512], 'vs', w=["Vs"])
                        b0_tasks = []
                        for g in range(8):
                            def load_k(g=g):
                                dma('pool', kst[g % 2][:, :, :], ckv[:, g * 4:(g + 1) * 4, hg * 512:(hg + 1) * 512], f"kst{g % 2}", w=[f"kst{g % 2}"])
                            for half in range(2):
                                def task(g=g, half=half, load_k=load_k):
                                    i2 = g % 2
                                    if half == 0:
                                        load_k()
                                    for j in range(4):
                                        for hh in range(2):
                                            h = 2 * half + hh
                                            tr(pT2[:, hh, j * 128:(j + 1) * 128], kst[i2][:, j, h * 128:(h + 1) * 128],
                                               r=[f"kst{i2}", "ident"], w=["pT2"])
                                    cp('dve', kTs[:, 2 * half:2 * half + 2, g * 512:(g + 1) * 512], pT2[:, :, :], r=["pT2"], w=["kTs"])
                                b0_tasks.append(task)

                        units = []
                        seqs = []
                        for s in range(2):
                            top = 4 * s * 4 + 15 if s == 1 else 15
                            top = 15 if s == 0 else 31
                            for h in range(4):
                                sq = len(seqs)
                                seqs.append(dict(kind='p', s=s, h=h))
                                for j in range(top, -1, -1):
                                    i = j - (top - 3)
                                    if i == 3:
                                        c0, mask = 0, (mask3b[:, :], 0, 512)
                                    elif i >= 0:
                                        c0, mask = 128 * i, (masktrib[:, :], 0, 128)
                                    else:
                                        c0, mask = 0, None
                                    n = 512 - c0
                                    units.append(dict(seq=sq, first=(j == top), last=(j == 0), c0=c0, n=n, mask=mask,
                                                      heads=[(kT[:, h, j * 128:(j + 1) * 128], qT[:, h, s * 512 + c0: s * 512 + 512],
                                                              Vc[:, j, h * 128:(h + 1) * 128], 0, n)]))
                        sq = len(seqs)
                        seqs.append(dict(kind='s'))
                        for j in range(32, -1, -1):
                            if j == 32:
                                heads = [(knew[:, hg * 4 + h, :], qT[:, h, 1024:1088], vnew[:, (hg * 4 + h) * 128:(hg * 4 + h + 1) * 128], h * 64, 64)
                                         for h in range(4)]
                                mask = (masks4b[:, :], 0, 256)
                            else:
                                heads = [(kTs[:, h, j * 128:(j + 1) * 128], qT[:, h, 1024:1088], Vs[:, j, h * 128:(h + 1) * 128], h * 64, 64)
                                         for h in range(4)]
                                mask = None
                            units.append(dict(seq=sq, first=(j == 32), last=(j == 0), c0=0, n=256, mask=mask, heads=heads))

                        seqpos = {}

                        def stage1(ui):
                            u = units[ui]; b = ui % 2
                            c0, n = u['c0'], u['n']
                            nh = len(u['heads'])
                            xr = ["kTs"] if seqs[u['seq']]['kind'] == 's' else []
                            for i, (ka, qa, va, col, hn_) in enumerate(u['heads']):
                                mm(pS[b][:, c0 + col:c0 + col + hn_], ka, qa, i == 0, (i == nh - 1) and u['mask'] is None,
                                   r=xr, w=[f"pS{b}"])
                            if u['mask'] is not None:
                                ma, mc, mn = u['mask']
                                mm(pS[b][:, c0 + mc:c0 + mc + mn], identb[:, :], ma, False, True, w=[f"pS{b}"])
                            act(e32[b][:, c0:c0 + n], pS[b][:, c0:c0 + n], AF.Exp, r=[f"pS{b}"], w=[f"e32{b}"])
                            act(spb[b][:, c0:c0 + n], e32[b][:, c0:c0 + n], AF.Ln, r=[f"e32{b}"], w=[f"spb{b}"], bias=1.0)
                            a = u['seq'] % 2
                            pos = seqpos.get(u['seq'], 0)
                            u['pos'] = pos
                            seqpos[u['seq']] = pos + 1
                            if not u['last']:
                                if u['first']:
                                    cp('dve', acc32[a][:, c0:c0 + n], spb[b][:, c0:c0 + n], r=[f"spb{b}"], w=[f"acc32{a}"])
                                else:
                                    tt('dve', acc32[a][:, c0:c0 + n], acc32[a][:, c0:c0 + n], spb[b][:, c0:c0 + n], ALU.add,
                                       r=[f"spb{b}", f"acc32{a}"], w=[f"acc32{a}"])
                                W = 512 if seqs[u['seq']]['kind'] == 'p' else 256
                                cp("dve", accbf[ui % 3][:, :W], acc32[a][:, :W], r=[f"acc32{a}"], w=[f"accbf{ui % 3}"])

                        def stage2(ui):
                            u = units[ui]; b = ui % 2
                            c0, n = u['c0'], u['n']
                            xr = ["kTs"] if seqs[u['seq']]['kind'] == 's' else []
                            for i, (ka, qa, va, col, hn_) in enumerate(u['heads']):
                                mm(pW[b][:, c0 + col:c0 + col + hn_], ka, qa, i == 0, False, r=xr, w=[f"pW{b}"])
                            if u['mask'] is not None:
                                ma, mc, mn = u['mask']
                                mm(pW[b][:, c0 + mc:c0 + mc + mn], identb[:, :], ma, False, False, w=[f"pW{b}"])
                            mm(pW[b][:, c0:c0 + n], negTb[:, :], spb[b][:, c0:c0 + n], False, u['first'], r=[f"spb{b}"], w=[f"pW{b}"])
                            if not u['first']:
                                pp = (ui - 1) % 3
                                mm(pW[b][:, c0:c0 + n], negonesb[:, :], accbf[pp][:, c0:c0 + n], False, True,
                                   r=[f"accbf{pp}"], w=[f"pW{b}"])
                            act(wtb[b][:, c0:c0 + n], pW[b][:, c0:c0 + n], AF.Exp, r=[f"pW{b}"], w=[f"wtb{b}"])

                        def stage3(ui):
                            u = units[ui]; b = ui % 2
                            c0, n = u['c0'], u['n']
                            o = u['seq'] % 2
                            nh = len(u['heads'])
                            for i, (ka, qa, va, col, hn_) in enumerate(u['heads']):
                                mm(pO[o][:, c0 + col:c0 + col + hn_], va, wtb[b][:, c0 + col:c0 + col + hn_],
                                   u['first'] and i == 0, u['last'] and i == nh - 1,
                                   r=[f"wtb{b}"] + (["Vs"] if seqs[u['seq']]['kind'] == 's' else []), w=[f"pO{o}"])
                            if u['last'] and 'e' not in _os.environ.get("BOFF", ""):
                                sq_ = seqs[u['seq']]
                                first_acc = (hg == 0 and sq_.get('h', 0) == 0)
                                if sq_['kind'] == 'p':
                                    s, h = sq_['s'], sq_['h']
                                    hgl = hg * 4 + h
                                    act(osq[:, :], pO[o][:, :], AF.Square, r=[f"pO{o}"], w=["osq"])
                                    ts('dve', mixA[:, hgl, s * 512:(s + 1) * 512], pO[o][:, :], gaT[:, hgl:hgl + 1], ALU.mult,
                                       r=[f"pO{o}"], w=[f"mixA{hgl}_{s}"])
                                    cp("dve", osqh[:, :], osq[:, :], r=["osq"], w=["osqh"])
                                    tt('dve', osql[:, :], osq[:, :], osqh[:, :], ALU.subtract, r=["osq", "osqh"], w=["osql"])
                                    for m4 in range(4):
                                        mm(pM[:, m4 * 128:(m4 + 1) * 128], osqh[:, m4 * 128:(m4 + 1) * 128], onesm[:, :], m4 == 0, False, r=["osqh"], w=["pM"])
                                    for m4 in range(4):
                                        mm(pM[:, m4 * 128:(m4 + 1) * 128], osql[:, m4 * 128:(m4 + 1) * 128], onesm[:, :], False, m4 == 3, r=["osql"], w=["pM"])
                                    pMc = pM[:, :].rearrange("p (a b) -> p a b", b=128)[:, :, 0]
                                    if first_acc:
                                        cp('dve', ssq_a[:, 4 * s:4 * s + 4], pMc, r=["pM"], w=["ssq_a"])
                                    else:
                                        tt('dve', ssq_a[:, 4 * s:4 * s + 4], ssq_a[:, 4 * s:4 * s + 4], pMc, ALU.add, r=["pM", "ssq_a"], w=["ssq_a"])
                                else:
                                    act(osq[:, :256], pO[o][:, :256], AF.Square, r=[f"pO{o}"], w=["osq"])
                                    for h in range(4):
                                        hgl = hg * 4 + h
                                        ts('dve', mixA[:, hgl, 1024:1088], pO[o][:, h * 64:(h + 1) * 64], gaT[:, hgl:hgl + 1], ALU.mult,
                                           r=[f"pO{o}"], w=[f"mixA{hgl}_2"])
                                    cp("dve", osqh[:, :256], osq[:, :256], r=["osq"], w=["osqh"])
                                    tt('dve', osql[:, :256], osq[:, :256], osqh[:, :256], ALU.subtract, r=["osq", "osqh"], w=["osql"])
                                    for h in range(4):
                                        mm(pM[:64, 0:128], osqh[:, h * 64:(h + 1) * 64], onesm[:, :], h == 0, False, r=["osqh"], w=["pM"])
                                    for h in range(4):
                                        mm(pM[:64, 0:128], osql[:, h * 64:(h + 1) * 64], onesm[:, :], False, h == 3, r=["osql"], w=["pM"])
                                    if hg == 0:
                                        cp('dve', ssq_a[:64, 8:9], pM[:64, 0:1], r=["pM"], w=["ssq_a"])
                                    else:
                                        tt('dve', ssq_a[:64, 8:9], ssq_a[:64, 8:9], pM[:64, 0:1], ALU.add, r=["pM", "ssq_a"], w=["ssq_a"])

                        NU = len(units)
                        if _os.environ.get("BLIMIT"):
                            NU = int(_os.environ["BLIMIT"])
                        _off = _os.environ.get("BOFF", "")
                        for it in range(NU + 2):
                            if it % 8 == 4 and b0_tasks:
                                b0_tasks.pop(0)()
                            if it < NU:
                                stage1(it)
                            if 0 <= it - 1 < NU and '2' not in _off:
                                stage2(it - 1)
                            if 0 <= it - 2 < NU and '3' not in _off and '2' not in _off:
                                stage3(it - 2)
                        while b0_tasks:
                            b0_tasks.pop(0)()
                        S.flush(blk)
                if stop_after == 'B' and hg == 0:
                    return nc

        with contextlib.ExitStack() as esC:
            x1 = sb("x1", [128, 9, D], F32, esC)
            hTo = sb("hTo", [128, 16, NOWN], BF, esC)
            mblocks = [(m, 128) for m in range(8)] + [(8, 64)]
            tiles = [(0, 512), (512, 512), (1024, 64)]

            def xsrc(m):
                if m < 4:
                    return d["x_ctx"][(12 + m) * 128:(13 + m) * 128, :]
                if m < 8:
                    return d["x_ctx"][(24 + m) * 128:(25 + m) * 128, :]
                return d["x_smp"][:, :]

            def norm_T(esX, blk, gname, dstT, load, halo=None):
                gb = sb("gb_" + gname, [128, D], F32, esX)
                hn = [sb(f"hnc{i}", [128, D], BF, esX) for i in range(2)]
                ss = [sb(f"ssc{i}", [128, 1], F32, esX) for i in range(2)]
                lnv = [sb(f"lnvc{i}", [128, 1], F32, esX) for i in range(2)]
                rs = [sb(f"rsc{i}", [128, 1], F32, esX) for i in range(2)]
                pT = ps("pTc", [128, 16, 128], BF, esX)
                dma('sp', gb[:], d[gname][:, :], 'c0', w=["gb"])
                items = [(x1[:R, m, :], R, dstT[:, :, m * 128:m * 128 + R], f"x1_{m}", m) for m, R in mblocks]
                if halo is not None:
                    items.append(halo)
                if load:
                    for idx, (xa, R, dst, xkey, m) in enumerate(items):
                        if m is not None:
                            dma('sp', xa, xsrc(m), f"x1l{idx}", w=[xkey])

                def fe1(idx):
                    xa, R, dst, xkey, m = items[idx]
                    i2 = idx % 2
                    act(hn[i2][:R, :], xa, AF.Square, r=[xkey], w=[f"hnc{i2}", f"ssc{i2}"], accum_out=ss[i2][:R, :])
                    rstd_ops(ss[i2][:R, :], lnv[i2][:R, :], rs[i2][:R, :], 1.0 / D, [f"ssc{i2}", f"lnvc{i2}", f"rsc{i2}"])
                    stt(hn[i2][:R, :], xa, rs[i2][:R, :], gb[:R, :], ALU.mult, ALU.mult, r=[xkey, f"rsc{i2}", "gb"], w=[f"hnc{i2}"])

                fe1(0)
                for idx, (xa, R, dst, xkey, m) in enumerate(items):
                    i2 = idx % 2
                    if idx + 1 < len(items):
                        fe1(idx + 1)
                    for kc in range(16):
                        tr(pT[:, kc, :R], hn[i2][:R, kc * 128:(kc + 1) * 128], r=[f"hnc{i2}", "ident"], w=["pTc"])
                    cp('act', dst, pT[:, :, :R], r=["pTc"], w=["dstT"])

            with contextlib.ExitStack() as esC12:
                hTh = sb("hTh", [128, 16, 4], BF, esC12)
                mixC = sb("mixC", [128, 8, NOWN], BF, esC12)
                with contextlib.ExitStack() as esX, nc.Block() as blk:
                    xh = sb("xh", [4, D], F32, esX)
                    dma('sp', xh[0:2, :], d["x_ctx"][1534:1536, :], 'xh', w=["xh"])
                    dma('sp', xh[2:4, :], d["x_ctx"][3582:3584, :], 'xh', w=["xh"])
                    norm_T(esX, blk, "g1bc", hTo, True, halo=(xh[:, :], 4, hTh[:, :, :], "xh", None))
                    S.flush(blk)
                with contextlib.ExitStack() as esX, nc.Block() as blk:
                    wc = [sb(f"wc{i}", [128, 3, 16, 128], BF, esX) for i in range(2)]
                    wst = [sb(f"wst{i}", [128, 16, 128], F32, esX) for i in range(2)]
                    u32 = [sb(f"u32{i}", [128, 516], F32, esX) for i in range(2)]
                    hc32 = [sb(f"hc32{i}", [128, 512], F32, esX) for i in range(1)] * 2
                    t1 = [sb(f"t1{i}", [128, 512], F32, esX) for i in range(2)]
                    oc32 = [sb(f"oc32{i}", [128, 512], F32, esX) for i in range(2)]
                    osqc = [sb(f"osqc{i}", [128, 512], F32, esX) for i in range(1)] * 2
                    osqch = [sb(f"osqch{i}", [128, 512], BF, esX) for i in range(1)] * 2
                    osqcl = [sb(f"osqcl{i}", [128, 512], BF, esX) for i in range(1)] * 2
                    hh = sb("hh", [128, 4], F32, esX); uh = sb("uh", [128, 4], F32, esX)
                    cpT = sb("cpT", [128, 16], F32, esX); csT = sb("csT", [128, 16], F32, esX)
                    pG = [[ps(f"pG{i}{j}", [128, 512], F32, esX) for j in range(3)] for i in range(2)]
                    pH = ps("pH", [128, 8], F32, esX)
                    pQ = ps("pQ", [128, 16], F32, esX)
                    cnt = 0
                    pend = [None]
                    def stage_w(g):
                        wi = g % 2
                        for j3, cbase in enumerate((3072, 4096, 5120)):
                            si = (3 * g + j3) % 2
                            dma('sp', wst[si][:, :, :], w_in_v[:, :, cbase + g * 128: cbase + (g + 1) * 128], f"wst{si}", w=[f"wst{si}"])
                            cp('act', wc[wi][:, j3, :, :], wst[si][:, :, :], r=[f"wst{si}"], w=[f"wc{wi}"])

                    stage_w(0)
                    for g in range(8):
                        wi = g % 2
                        if g + 1 < 8:
                            stage_w(g + 1)
                        for j3 in (1, 2):
                            for kc in range(16):
                                mm(pH[:, (j3 - 1) * 4:(j3 - 1) * 4 + 4], wc[wi][:, j3, kc, :], hTh[:, kc, :], kc == 0 and j3 == 1, kc == 15 and j3 == 2,
                                   r=[f"wc{wi}"], w=["pH"])
                        cp('act', hh[:, :], pH[:, 4:8], r=["pH"], w=["hh"])
                        tt('dve', uh[:, :], pH[:, 0:4], hh[:, :], ALU.mult, r=["pH", "hh"], w=["uh"])
                        for ti, (T0, n) in enumerate(tiles):
                            b = cnt % 2; cnt += 1
                            for j3 in range(3):
                                for kc in range(16):
                                    mm(pG[b][j3][:, :n], wc[wi][:, j3, kc, :], hTo[:, kc, T0:T0 + n], kc == 0, kc == 15, r=[f"wc{wi}"], w=[f"pG{b}{j3}"])
                            if pend[0] is not None:
                                pend[0]()
                                pend[0] = None
                            U = f"u32{b}"
                            cp('act', hc32[b][:, :n], pG[b][2][:, :n], r=[f"pG{b}2"], w=["hc32"])
                            tt('dve', u32[b][:, 2:2 + n], pG[b][1][:, :n], hc32[b][:, :n], ALU.mult, r=[f"pG{b}1", "hc32"], w=[U])
                            if ti < 2:
                                cp('dve', u32[b][:, 0:2], uh[:, 2 * ti:2 * ti + 2], r=["uh"], w=[U])
                            else:
                                cp('dve', u32[b][:, 0:2], scT[:, 2 * g:2 * g + 2], w=[U])
                            ts('dve', t1[b][:, :n], u32[b][:, 2:2 + n], cwT[:, 3 * g + 2:3 * g + 3], ALU.mult, r=[U], w=[f"t1{b}"])
                            stt(t1[b][:, :n], u32[b][:, 1:1 + n], cwT[:, 3 * g + 1:3 * g + 2], t1[b][:, :n], ALU.mult, ALU.add, r=[U, f"t1{b}"], w=[f"t1{b}"])
                            stt(t1[b][:, :n], u32[b][:, 0:n], cwT[:, 3 * g:3 * g + 1], t1[b][:, :n], ALU.mult, ALU.add, r=[U, f"t1{b}"], w=[f"t1{b}"])
                            tt('dve', oc32[b][:, :n], pG[b][0][:, :n], t1[b][:, :n], ALU.mult, r=[f"pG{b}0", f"t1{b}"], w=[f"oc32{b}"])
                            ts('dve', mixC[:, g, T0:T0 + n], oc32[b][:, :n], gcT[:, g:g + 1], ALU.mult, r=[f"oc32{b}"], w=["mixC"])
                            act(osqc[b][:, :n], oc32[b][:, :n], AF.Square, r=[f"oc32{b}"], w=["osqc"])
                            if ti == 1:
                                cp('pool', cpT[:, 2 * g:2 * g + 2], u32[b][:, 512:514], r=[U], w=["cpT"])
                            if ti == 2:
                                cp('pool', csT[:, 2 * g:2 * g + 2], u32[b][:, 64:66], r=[U], w=["csT"])
                            nw = 128 * ((n + 127) // 128)
                            cp('act', osqch[b][:, :nw], osqc[b][:, :nw], r=["osqc"], w=["osqch"])
                            tt('dve', osqcl[b][:, :nw], osqc[b][:, :nw], osqch[b][:, :nw], ALU.subtract, r=["osqc", "osqch"], w=["osqcl"])
                            def ssq_mms(g=g, n=n, T0=T0, b=b):
                                for m4 in range((n + 127) // 128):
                                    R = 128
                                    m = T0 // 128 + m4
                                    mm(pQ[:R, m:m + 1], osqch[b][:, m4 * 128:m4 * 128 + R], onesb[:, :], g == 0 and m == 0, False,
                                       r=["osqch"], w=["pQ"])
                                    mm(pQ[:R, m:m + 1], osqcl[b][:, m4 * 128:m4 * 128 + R], onesb[:, :], False, g == 7 and m == 8,
                                       r=["osqcl"], w=["pQ"])
                            pend[0] = ssq_mms
                    if pend[0] is not None:
                        pend[0]()
                    cp('dve', ssq_c[:, :], pQ[:, 0:9], r=["pQ"], w=["ssq_c"])
                    dma('sp', d["conv_pT"][:, :], cpT[:, :], 'cvo', r=["cpT"])
                    dma('sp', d["conv_sT"][:, :], csT[:, :], 'cvo', r=["csT"])
                    S.flush(blk)
                if stop_after == 'C1':
                    return nc
                with contextlib.ExitStack() as esX, nc.Block() as blk:
                    wo = [sb(f"wo{i}", [128, 16, 512], BF, esX) for i in range(2)]
                    pA = [ps(f"pA{i}", [128, 512], F32, esX) for i in range(2)]
                    pC = [ps(f"pC{i}", [128, 512], F32, esX) for i in range(2)]
                    g2b = sb("g2b", [128, D], F32, esX)
                    hn2 = [sb(f"hn2{i}", [128, D], BF, esX) for i in range(2)]
                    ss2 = [sb(f"ss2{i}", [128, 1], F32, esX) for i in range(2)]
                    ln2 = [sb(f"ln2{i}", [128, 1], F32, esX) for i in range(2)]
                    rs2 = [sb(f"rs2{i}", [128, 1], F32, esX) for i in range(2)]
                    pT2n = ps("pT2n", [128, 16, 128], BF, esX)
                    dma('sp', g2b[:], d["g2bc"][:, :], 'c0', w=["g2b"])

                    def n2_fe1(m, R):
                        i2 = m % 2
                        xk = [f"x1_{m}_{k}" for k in range(4)]
                        act(hn2[i2][:R, :], x1[:R, m, :], AF.Square, r=xk, w=[f"hn2{i2}", f"ss2{i2}"], accum_out=ss2[i2][:R, :])
                        rstd_ops(ss2[i2][:R, :], ln2[i2][:R, :], rs2[i2][:R, :], 1.0 / D, [f"ss2{i2}", f"ln2{i2}", f"rs2{i2}"])
                        stt(hn2[i2][:R, :], x1[:R, m, :], rs2[i2][:R, :], g2b[:R, :], ALU.mult, ALU.mult, r=xk + [f"rs2{i2}", "g2b"], w=[f"hn2{i2}"])

                    def n2_fe2(m, R):
                        i2 = m % 2
                        for kc in range(16):
                            tr(pT2n[:, kc, :R], hn2[i2][:R, kc * 128:(kc + 1) * 128], r=[f"hn2{i2}", "ident"], w=["pT2n"])
                        cp('act', hTo[:, :, m * 128:m * 128 + R], pT2n[:, :, :R], r=["pT2n"], w=["h2T"])
                    S.op('pool', lambda e: e.memset(ssq_a[64:128, 8:9], 1.0), w=["ssq_a"])
                    S.op('pool', lambda e: e.memset(ssq_c[64:128, 8:9], 1.0), w=["ssq_c"])
                    rstd_ops(ssq_a[:, :], lna[:, :], ra[:, :], 1.0 / 1024, ["ssq_a", "lna", "ra"])
                    rstd_ops(ssq_c[:, :], lnc[:, :], rc[:, :], 1.0 / 1024, ["ssq_c", "lnc", "rc"])
                    w_out_v = d["w_out"].rearrange("(kc p) n -> p kc n", p=128)
                    cnt = 0
                    for nn in range(4):
                        wi = nn % 2
                        for half in range(2):
                            dma('pool', wo[wi][:, half * 8:(half + 1) * 8, :], w_out_v[:, half * 8:(half + 1) * 8, nn * 512:(nn + 1) * 512], f"wo{wi}", w=[f"wo{wi}"])
                        for m, R in mblocks:
                            b = cnt % 2; cnt += 1
                            for kc in range(8):
                                mm(pA[b][:R, :], mixA[:, kc, m * 128:m * 128 + R], wo[wi][:, kc, :], kc == 0, kc == 7, r=[f"wo{wi}"], w=[f"pA{b}"])
                            for kc in range(8):
                                mm(pC[b][:R, :], mixC[:, kc, m * 128:m * 128 + R], wo[wi][:, 8 + kc, :], kc == 0, kc == 7, r=[f"wo{wi}"], w=[f"pC{b}"])
                            xs = x1[:R, m, nn * 512:(nn + 1) * 512]
                            stt(xs, pA[b][:R, :], ra[:R, m:m + 1], xs, ALU.mult, ALU.add, r=[f"pA{b}", "ra", f"x1_{m}_{nn}"], w=[f"x1_{m}_{nn}"])
                            stt(xs, pC[b][:R, :], rc[:R, m:m + 1], xs, ALU.mult, ALU.add, r=[f"pC{b}", "rc", f"x1_{m}_{nn}"], w=[f"x1_{m}_{nn}"])
                            if nn == 3 and stop_after != 'C2':
                                n2_fe1(m, R)
                                if m >= 1:
                                    n2_fe2(m - 1, mblocks[m - 1][1])
                    if stop_after != 'C2':
                        n2_fe2(8, 64)
                    S.flush(blk)
                if stop_after == 'C2':
                    with nc.Block() as blk:
                        for m, R in mblocks:
                            if m < 8:
                                dma('sp', d["y_p"][m * 128:(m + 1) * 128, :], x1[:, m, :], 'yo')
                            else:
                                dma('sp', d["y_s"][:, :], x1[:64, m, :], 'yo')
                        dma('sp', d["dbgA"][:, :], mixA[:, :, :].rearrange("p h t -> p (h t)"), 'yo')
                        dma('sp', d["dbgC"][:, :], mixC[:, :, :].rearrange("p h t -> p (h t)"), 'yo')
                        dma('sp', d["dbgS"][:, 0:9], ssq_a[:, :], 'yo')
                        dma('sp', d["dbgS"][:, 9:18], ssq_c[:, :], 'yo')
                        S.flush(blk)
            if stop_after == 'C2':
                return nc
            with contextlib.ExitStack() as esC34:
                h2T = hTo
                with contextlib.ExitStack() as esX, nc.Block() as blk:
                    GW = 256
                    NG = DFF // GW
                    wg = [sb(f"wg{i}", [128, 16, GW], BF, esX) for i in range(2)]
                    wu = [sb(f"wu{i}", [128, 16, GW], BF, esX) for i in range(2)]
                    wd = [sb(f"wd{i}", [128, 2, D], BF, esX) for i in range(2)]
                    aT = [sb(f"aT{i}", [128, 2, NOWN], BF, esX) for i in range(2)]
                    sg = [sb(f"sg{i}", [128, 512], F32, esX) for i in range(2)]
                    pGt = [ps(f"pGt{i}", [128, 512], F32, esX) for i in range(2)]
                    pUt = [ps(f"pUt{i}", [128, 512], F32, esX) for i in range(2)]
                    pD = [ps(f"pD{i}", [128, 512], F32, esX) for i in range(4)]
                    wgv = d["w_gate"].rearrange("(kc p) n -> p kc n", p=128)
                    wuv = d["w_up"].rearrange("(kc p) n -> p kc n", p=128)
                    wdv = d["w_down"].rearrange("(f p) n -> p f n", p=128)
                    cg = [0]; cd = [0]

                    def load_w(G):
                        wi = G % 2
                        dma('pool', wg[wi][:, :, :], wgv[:, :, G * GW:(G + 1) * GW], f"wg{wi}", w=[f"wg{wi}"])
                        dma('pool', wu[wi][:, :, :], wuv[:, :, G * GW:(G + 1) * GW], f"wu{wi}", w=[f"wu{wi}"])

                    def load_wd(G):
                        wi = G % 2
                        dma('pool', wd[wi][:, :, :], wdv[:, 2 * G:2 * G + 2, :], f"wd{wi}", w=[f"wd{wi}"])

                    def gateup(G, f, T0, n):
                        wi = G % 2
                        b = cg[0] % 2; cg[0] += 1
                        for kc in range(16):
                            mm(pGt[b][:, :n], wg[wi][:, kc, f * 128:(f + 1) * 128], h2T[:, kc, T0:T0 + n], kc == 0, kc == 15, r=[f"wg{wi}"], w=[f"pGt{b}"])
                        for kc in range(16):
                            mm(pUt[b][:, :n], wu[wi][:, kc, f * 128:(f + 1) * 128], h2T[:, kc, T0:T0 + n], kc == 0, kc == 15, r=[f"wu{wi}"], w=[f"pUt{b}"])
                        act(sg[b][:, :n], pGt[b][:, :n], AF.Silu, r=[f"pGt{b}"], w=[f"sg{b}"])
                        tt('dve', aT[wi][:, f, T0:T0 + n], pUt[b][:, :n], sg[b][:, :n], ALU.mult, r=[f"pUt{b}", f"sg{b}"], w=[f"aT{wi}"])

                    def down(G, m, R, nn):
                        wi = G % 2
                        b = cd[0] % 4; cd[0] += 1
                        for f in range(2):
                            mm(pD[b][:R, :], aT[wi][:, f, m * 128:m * 128 + R], wd[wi][:, f, nn * 512:(nn + 1) * 512], f == 0, f == 1,
                               r=[f"aT{wi}", f"wd{wi}"], w=[f"pD{b}"])
                        xs = x1[:R, m, nn * 512:(nn + 1) * 512]
                        tt('dve', xs, xs, pD[b][:R, :], ALU.add, r=[f"pD{b}", f"x1_{m}_{nn}"], w=[f"x1_{m}_{nn}"])
                        if G == NG - 1 and nn == 3:
                            if m < 8:
                                dma('sp', d["y_p"][m * 128:(m + 1) * 128, :], x1[:, m, :], 'yo', r=[f"x1_{m}_{k}" for k in range(4)])
                            else:
                                dma('sp', d["y_s"][:, :], x1[:64, m, :], 'yo', r=[f"x1_{m}_{k}" for k in range(4)])

                    load_w(0)
                    load_wd(0)
                    for G in range(NG + 1):
                        if G + 1 < NG:
                            load_w(G + 1)
                        gu = [(f, T0, n) for f in range(2) for (T0, n) in tiles] if G < NG else []
                        dn = [(m, R, nn) for (m, R) in mblocks for nn in range(4)] if G >= 1 else []
                        per = (len(dn) + 5) // 6
                        for i in range(6):
                            if i < len(gu):
                                gateup(G, *gu[i])
                            for (m, R, nn) in dn[i * per:(i + 1) * per]:
                                down(G - 1, m, R, nn)
                        if G + 1 < NG:
                            load_wd(G + 1)
                    S.flush(blk)
    return nc


_NC_CACHE = {}


def _consts():
    bf = ml_dtypes.bfloat16
    k = np.arange(128)[:, None]
    q = np.arange(128)[None, :]
    ident = np.eye(128, dtype=np.float32).astype(bf)
    negT = np.where(k >= q, -1.0, 0.0).astype(np.float32).astype(bf)
    negones = np.full((128, 128), -1.0, np.float32).astype(bf)
    masktri = np.where(k >= q, NEG, 0.0).astype(np.float32)
    mask3 = np.full((128, 512), NEG, np.float32)
    mask3[:, 384:] = masktri
    q64 = np.arange(64)[None, :]
    ms = np.where((k >= q64) | (k >= 64), NEG, 0.0).astype(np.float32)
    masks4 = np.tile(ms, (1, 4))
    return dict(ident=ident, negT=negT, negones=negones, masktri=masktri.astype(bf), mask3=mask3.astype(bf),
                masks4=masks4.astype(bf), ones32=np.ones((128, 1), np.float32), onesb=np.ones((128, 1), np.float32).astype(bf),
                onesm=np.ones((128, 128), np.float32).astype(bf))


def kernel(x_prompt, x_sample, cache_k, cache_v, state_conv, g_norm1, w_in, g_q, g_k, conv_w,
           g_attn_out, g_conv_out, w_out, g_norm2, w_gate, w_up, w_down):
    f = lambda a: np.ascontiguousarray(np.asarray(a, dtype=np.float32))
    x_prompt, x_sample, cache_k, cache_v, state_conv = map(f, (x_prompt, x_sample, cache_k, cache_v, state_conv))
    if 'nc' not in _NC_CACHE:
        _NC_CACHE['nc'] = build()
    nc = _NC_CACHE['nc']
    cst = _consts()
    shared = dict(
        w_in=f(w_in)[0], w_out=f(w_out)[0], w_gate=f(w_gate)[0], w_up=f(w_up)[0], w_down=f(w_down)[0],
        g1bc=np.ascontiguousarray(np.broadcast_to(f(g_norm1)[0][None, :], (128, D))),
        g2bc=np.ascontiguousarray(np.broadcast_to(f(g_norm2)[0][None, :], (128, D))),
        gqbc=np.ascontiguousarray(np.broadcast_to(f(g_q)[0][None, :], (128, 128))),
        gkbc=np.ascontiguousarray(np.broadcast_to(f(g_k)[0][None, :], (128, 128))),
        gaT=np.ascontiguousarray(f(g_attn_out)[0].reshape(8, 128).T),
        gcT=np.ascontiguousarray(f(g_conv_out)[0].reshape(8, 128).T),
        cwT=np.ascontiguousarray(f(conv_w)[0].reshape(3, 8, 128).transpose(2, 1, 0).reshape(128, 24)),
        **cst)
    in_maps = []
    for c in range(8):
        b, r = c // 4, c % 4
        xc = np.zeros((4096, D), np.float32)
        xc[(3 - r) * 512:] = x_prompt[b, :(5 + r) * 512]
        m = dict(shared)
        m.update(x_ctx=xc, x_smp=x_sample[c], ck=cache_k[0, c].reshape(4096, 1024), cv=cache_v[0, c].reshape(4096, 1024),
                 scT=np.ascontiguousarray(state_conv[0, c].reshape(2, 8, 128).transpose(2, 1, 0).reshape(128, 16)))
        in_maps.append(m)
    res = run_bass_kernel_spmd(nc, in_maps, core_ids=list(range(8)))
    R = res.results
    y_p = np.zeros((2, 4096, D), np.float32); k_p = np.zeros((1, 2, 4096, 8, 128), np.float32); v_p = np.zeros_like(k_p)
    y_s = np.zeros((8, 64, D), np.float32); k_s = np.zeros((1, 8, 64, 8, 128), np.float32); v_s = np.zeros_like(k_s)
    c_p = np.zeros((1, 2, 2, 1024), np.float32); c_s = np.zeros((1, 8, 2, 1024), np.float32)
    for c in range(8):
        b, r = c // 4, c % 4
        o = R[c]
        for s, ch in enumerate((r, 4 + r)):
            y_p[b, ch * 512:(ch + 1) * 512] = o["y_p"][s * 512:(s + 1) * 512]
            k_p[0, b, ch * 512:(ch + 1) * 512] = o["k_p"][s * 512:(s + 1) * 512].reshape(512, 8, 128)
            v_p[0, b, ch * 512:(ch + 1) * 512] = o["v_p"][s * 512:(s + 1) * 512].reshape(512, 8, 128)
        y_s[c] = o["y_s"]
        k_s[0, c] = o["k_s"].reshape(64, 8, 128); v_s[0, c] = o["v_s"].reshape(64, 8, 128)
        c_s[0, c] = o["conv_sT"].reshape(128, 8, 2).transpose(2, 1, 0).reshape(2, 1024)
        if r == 3:
            c_p[0, b] = o["conv_pT"].reshape(128, 8, 2).transpose(2, 1, 0).reshape(2, 1024)
    return (y_p, y_s, k_p, v_p, c_p, k_s, v_s, c_s)
```

```python
import contextlib
import os as _os
import numpy as np
import ml_dtypes
import concourse.bass as bass
import concourse.mybir as mybir
from concourse.bass_utils import run_bass_kernel_spmd

F32 = mybir.dt.float32
BF = mybir.dt.bfloat16
AF = mybir.ActivationFunctionType
ALU = mybir.AluOpType
EPS = 1e-6
NEG = -30000.0
D = 2048
DFF = 5632
NOWN = 1088
CENG = ('pe', 'act', 'dve', 'pool')
SEG = 1000


class Sched:
    def __init__(self, nc, sem_pool):
        self.nc = nc
        self.sem_pool = list(sem_pool)
        self.engsem = {e: [] for e in CENG}
        self.dmasem = {}
        self.dmacount = {}
        self.sigcount = {e: 0 for e in CENG}
        self.waited = {}
        self.ops = []

    def op(self, eng, fn, r=(), w=(), dma=None):
        self.ops.append(dict(eng=eng, fn=fn, r=tuple(r), w=tuple(w), dma=dma, sig=False, waits=[]))

    def flush(self, block):
        ops = self.ops
        self.ops = []
        lastw, readers = {}, {}
        dcount = dict(self.dmacount)
        for i, o in enumerate(ops):
            deps = {}
            for k in o['r']:
                if k in lastw:
                    deps[lastw[k]] = True
                if k[0] == 'p' and k[1:2].isupper():
                    for rd in readers.get(k, ()):
                        if ops[rd]['eng'] != o['eng']:
                            deps.setdefault(rd, False)
            for k in o['w']:
                if k in lastw:
                    deps.setdefault(lastw[k], False)
                for rd in readers.get(k, ()):
                    deps.setdefault(rd, False)
            deps.pop(i, None)
            for p, raw in deps.items():
                po = ops[p]
                if po['dma'] is not None:
                    o['waits'].append(('dma', po['dma'], 16 * dcount[po['dma']]))
                    continue
                same = po['eng'] == o['eng']
                if same and o['dma'] is None and o['eng'] == 'pe':
                    continue
                po['sig'] = True
                o['waits'].append(('eng', po['eng'], p))
            if o['dma'] is not None:
                if o['dma'] not in self.dmasem:
                    self.dmasem[o['dma']] = self.sem_pool.pop(0)
                dcount[o['dma']] = dcount.get(o['dma'], 0) + 1
            for k in o['r']:
                readers.setdefault(k, []).append(i)
            for k in o['w']:
                lastw[k] = i
                readers[k] = []
        for o in ops:
            if o['sig']:
                c = self.sigcount[o['eng']]
                self.sigcount[o['eng']] = c + 1
                if c // SEG >= len(self.engsem[o['eng']]):
                    self.engsem[o['eng']].append(self.sem_pool.pop())
                o['sigsem'] = self.engsem[o['eng']][c // SEG]
                o['sigval'] = c % SEG + 1
        self.dmacount = dcount
        engmap = {'pe': block.tensor, 'act': block.scalar, 'dve': block.vector,
                  'pool': block.gpsimd, 'sp': block.sync}
        for ename, deco in engmap.items():
            mine = [o for o in ops if o['eng'] == ename]
            last = (ename == 'sp')
            if not mine and not last:
                continue

            def body(eng, mine=mine, ename=ename, last=last):
                for o in mine:
                    need = {}
                    for wt in o['waits']:
                        if wt[0] == 'dma':
                            sem, val = self.dmasem[wt[1]], wt[2]
                        else:
                            sem, val = ops[wt[2]]['sigsem'], ops[wt[2]]['sigval']
                        sid = id(sem)
                        if val > need.get(sid, (None, 0))[1]:
                            need[sid] = (sem, val)
                    for sid, (sem, val) in need.items():
                        if self.waited.get((ename, sid), 0) >= val:
                            continue
                        self.waited[(ename, sid)] = val
                        eng.wait_ge(sem, val)
                    ins = o['fn'](eng)
                    if o['sig']:
                        ins.then_inc(o['sigsem'], 1)
                    if o['dma'] is not None:
                        ins.then_inc(self.dmasem[o['dma']], 16)
                if last:
                    for key, sem in self.dmasem.items():
                        val = 16 * self.dmacount[key]
                        if val and self.waited.get((ename, id(sem)), 0) < val:
                            self.waited[(ename, id(sem))] = val
                            eng.wait_ge(sem, val)
            deco(body)


def build(stop_after=None, skip=()):
    nc = bass.Bass("TRN2", target_bir_lowering=False)
    d = {}

    def din(name, shape, dt=F32):
        d[name] = nc.dram_tensor(name, list(shape), dt, kind="ExternalInput").ap()

    def dout(name, shape, dt=F32):
        d[name] = nc.dram_tensor(name, list(shape), dt, kind="ExternalOutput").ap()

    din("x_ctx", [4096, D]); din("x_smp", [64, D]); din("ck", [4096, 1024]); din("cv", [4096, 1024])
    din("w_in", [D, 6144]); din("w_out", [D, D]); din("w_gate", [D, DFF]); din("w_up", [D, DFF])
    din("w_down", [DFF, D])
    din("g1bc", [128, D]); din("g2bc", [128, D]); din("gqbc", [128, 128]); din("gkbc", [128, 128])
    din("gaT", [128, 8]); din("gcT", [128, 8]); din("cwT", [128, 24]); din("scT", [128, 16])
    din("ident", [128, 128], BF); din("negT", [128, 128], BF); din("negones", [128, 128], BF)
    din("mask3", [128, 512], BF); din("masktri", [128, 128], BF); din("masks4", [128, 256], BF)
    din("ones32", [128, 1]); din("onesb", [128, 1], BF); din("onesm", [128, 128], BF)
    dout("y_p", [1024, D]); dout("y_s", [64, D]); dout("k_p", [1024, 1024]); dout("v_p", [1024, 1024])
    dout("k_s", [64, 1024]); dout("v_s", [64, 1024]); dout("conv_pT", [128, 16]); dout("conv_sT", [128, 16])

    if stop_after == 'C2':
        dout("dbgA", [128, 8 * NOWN], BF); dout("dbgC", [128, 8 * NOWN], BF); dout("dbgS", [128, 18])
    w_in_v = d["w_in"].rearrange("(kc p) n -> p kc n", p=128)

    with contextlib.ExitStack() as es:
        E = es.enter_context
        sems = [E(nc.semaphore(f"sm{i}")) for i in range(96)]
        S = Sched(nc, sems)

        uid = [0]

        def sb(name, shape, dt, stack=None):
            uid[0] += 1
            return (stack or es).enter_context(nc.sbuf_tensor(f"{name}_{uid[0]}", list(shape), dt))

        def ps(name, shape, dt, stack):
            uid[0] += 1
            return stack.enter_context(nc.psum_tensor(f"{name}_{uid[0]}", list(shape), dt))

        def mm(out, lhsT, rhs, start, stop, r=(), w=()):
            S.op('pe', lambda e: e.matmul(out, lhsT=lhsT, rhs=rhs, start=start, stop=stop), r=r, w=w)

        def tr(out, in_, r=(), w=()):
            S.op('pe', lambda e: e.transpose(out=out, in_=in_, identity=identb[:in_.shape[0], :in_.shape[0]]), r=r, w=w)

        def act(out, in_, func, r=(), w=(), **kw):
            S.op('act', lambda e: e.activation(out=out, in_=in_, func=func, **kw), r=r, w=w)

        def stt(out, in0, scalar, in1, op0, op1, r=(), w=()):
            S.op('dve', lambda e: e.scalar_tensor_tensor(out=out, in0=in0, scalar=scalar, in1=in1, op0=op0, op1=op1), r=r, w=w)

        def tt(eng, out, in0, in1, op, r=(), w=()):
            S.op(eng, lambda e: e.tensor_tensor(out=out, in0=in0, in1=in1, op=op), r=r, w=w)

        def ts(eng, out, in0, s1, op0, r=(), w=()):
            S.op(eng, lambda e: e.tensor_scalar(out=out, in0=in0, scalar1=s1, scalar2=None, op0=op0), r=r, w=w)

        def cp(eng, out, in_, r=(), w=()):
            if eng == 'act':
                act(out, in_, AF.Copy, r=r, w=w)
            else:
                S.op(eng, lambda e: e.tensor_copy(out=out, in_=in_), r=r, w=w)

        def dma(eng, out, in_, key, r=(), w=()):
            S.op(eng, lambda e: e.dma_start(out=out, in_=in_), r=r, w=w, dma=key)

        def rstd_ops(ss, lnv, rs, scale, keys, bias=EPS):
            act(lnv, ss, AF.Ln, r=[keys[0]], w=[keys[1]], scale=scale, bias=bias)
            act(rs, lnv, AF.Exp, r=[keys[1]], w=[keys[2]], scale=-0.5)

        identb = sb("identb", [128, 128], BF); negTb = sb("negTb", [128, 128], BF)
        negonesb = sb("negonesb", [128, 128], BF); mask3b = sb("mask3b", [128, 512], BF)
        masktrib = sb("masktrib", [128, 128], BF); masks4b = sb("masks4b", [128, 256], BF)
        ones32 = sb("ones32s", [128, 1], F32); onesb = sb("onesbs", [128, 1], BF); onesm = sb("onesms", [128, 128], BF)
        gaT = sb("gaTs", [128, 8], F32); gcT = sb("gcTs", [128, 8], F32)
        cwT = sb("cwTs", [128, 24], F32); scT = sb("scTs", [128, 16], F32)
        gqs = sb("gqs", [128, 128], F32); gkb = sb("gkb", [128, 128], F32)
        ssq_a = sb("ssq_a", [128, 9], F32); ssq_c = sb("ssq_c", [128, 9], F32)
        ra = sb("ra", [128, 9], F32); rc = sb("rc", [128, 9], F32)
        lna = sb("lna", [128, 9], F32); lnc = sb("lnc", [128, 9], F32)
        mixA = sb("mixA", [128, 8, NOWN], BF)
        knew = sb("knew", [128, 8, 128], BF)
        vnew = sb("vnew", [128, 1024], BF)

        own_of_ctx = {12 + i: i for i in range(4)}
        own_of_ctx.update({28 + i: 4 + i for i in range(4)})

        with nc.Block() as blk:
            for nm, t in (("ident", identb), ("negT", negTb), ("negones", negonesb), ("mask3", mask3b),
                          ("masktri", masktrib), ("masks4", masks4b), ("ones32", ones32), ("onesb", onesb), ("onesm", onesm), ("gaT", gaT),
                          ("gcT", gcT), ("cwT", cwT), ("scT", scT), ("gqbc", gqs), ("gkbc", gkb)):
                dma('sp', t[:], d[nm][:, :], 'c0', w=[nm])
            ts('dve', gqs[:], gqs[:], float(128.0 ** -0.5), ALU.mult, r=["gqbc"], w=["gqs"])
            S.op('pool', lambda e: e.memset(knew[:], 0.0), w=["knew"])
            S.op('pool', lambda e: e.memset(vnew[:], 0.0), w=["vnew"])
            S.flush(blk)

        with contextlib.ExitStack() as esAB:
            kT = sb("kTc", [128, 4, 4096], BF, esAB)
            Vc = sb("Vc", [128, 32, 512], BF, esAB)
            qT = sb("qTc", [128, 4, NOWN], BF, esAB)
            g1b = sb("g1b", [128, D], F32, esAB)
            for hg in range(2):
                with contextlib.ExitStack() as esA, nc.Block() as blk:
                    wq = sb("wq", [128, 16, 512], BF, esA); wk = sb("wk", [128, 16, 512], BF, esA)
                    wv = sb("wv", [128, 16, 512], BF, esA)
                    xb = [sb(f"xb{i}", [128, D], F32, esA) for i in range(2)]
                    hn = [sb(f"hn{i}", [128, D], BF, esA) for i in range(2)]
                    hnT = [sb(f"hnT{i}", [128, 16, 128], BF, esA) for i in range(2)]
                    ss = [sb(f"ss{i}", [128, 1], F32, esA) for i in range(2)]
                    lnv = [sb(f"lnv{i}", [128, 1], F32, esA) for i in range(2)]
                    rs = [sb(f"rs{i}", [128, 1], F32, esA) for i in range(2)]
                    ssk = [sb(f"ssk{i}", [128, 8], F32, esA) for i in range(2)]
                    lnk = [sb(f"lnk{i}", [128, 8], F32, esA) for i in range(2)]
                    rk = [sb(f"rk{i}", [128, 8], F32, esA) for i in range(2)]
                    k32 = [sb(f"k32{i}", [128, 512], F32, esA) for i in range(2)]
                    v32 = [sb(f"v32{i}", [128, 512], F32, esA) for i in range(2)]
                    kbf = [sb(f"kbf{i}", [128, 512], BF, esA) for i in range(2)]
                    q32 = sb("q32", [128, 512], F32, esA); qbf = sb("qbf", [128, 512], BF, esA)
                    junk = sb("junk", [128, 512], BF, esA)
                    pT = ps("pT", [128, 16, 128], BF, esA)
                    pKV = [ps(f"pKV{i}", [128, 512], F32, esA) for i in range(4)]
                    pKT = ps("pKT", [128, 8, 128], BF, esA)

                    if hg == 0:
                        dma('sp', g1b[:], d["g1bc"][:, :], 'c0', w=["g1b"])
                    for i, wt, c0 in ((1, wk, 1024), (2, wv, 2048), (0, wq, 0)):
                        for half in range(2):
                            dma('pool', wt[:, half * 8:(half + 1) * 8, :],
                                w_in_v[:, half * 8:(half + 1) * 8, c0 + hg * 512: c0 + hg * 512 + 512],
                                f'wA{i}{half}', w=[f"w{i}{'ab'[half]}"])
                    blocks = [(t, 128, d["x_ctx"][t * 128:(t + 1) * 128, :], own_of_ctx.get(t)) for t in range(32)]
                    blocks.append((32, 64, d["x_smp"][:, :], 8))

                    def load_x(bi):
                        t, R, src, om = blocks[bi]
                        dma('sp', xb[bi % 2][:R, :], src, f"xb{bi % 2}", w=[f"xb{bi % 2}"])

                    rotc = [0]

                    def FE(bi):
                        t, R, src, om = blocks[bi]
                        i2 = bi % 2
                        X, H, HT = f"xb{i2}", f"hn{i2}", f"hnT{i2}"
                        act(hn[i2][:R, :], xb[i2][:R, :], AF.Square, r=[X], w=[H, f"ss{i2}"], accum_out=ss[i2][:R, :])
                        rstd_ops(ss[i2][:R, :], lnv[i2][:R, :], rs[i2][:R, :], 1.0 / D, [f"ss{i2}", f"lnv{i2}", f"rs{i2}"])
                        stt(hn[i2][:R, :], xb[i2][:R, :], rs[i2][:R, :], g1b[:R, :], ALU.mult, ALU.mult,
                            r=[X, f"rs{i2}", "g1b"], w=[H])

                    def FE2(bi):
                        t, R, src, om = blocks[bi]
                        i2 = bi % 2
                        X, H, HT = f"xb{i2}", f"hn{i2}", f"hnT{i2}"
                        for kc in range(8):
                            tr(pT[:, kc, :R], hn[i2][:R, kc * 128:(kc + 1) * 128], r=[H, "ident"], w=["pTa"])
                        cp('dve', hnT[i2][:, 0:8, :R], pT[:, 0:8, :R], r=["pTa"], w=[HT + "a"])
                        for kc in range(8, 16):
                            tr(pT[:, kc, :R], hn[i2][:R, kc * 128:(kc + 1) * 128], r=[H, "ident"], w=["pTb"])
                        cp('act', hnT[i2][:, 8:16, :R], pT[:, 8:16, :R], r=["pTb"], w=[HT + "b"])

                    def BE(bi, mid=None):
                        t, R, src, om = blocks[bi]
                        i2 = bi % 2
                        X, H, HT = f"xb{i2}", f"hn{i2}", f"hnT{i2}"
                        rot = rotc[0]
                        own = om is not None
                        pk = rot % 4; rot += 1
                        for kc in range(16):
                            mm(pKV[pk][:R, :], hnT[i2][:, kc, :R], wk[:, kc, :], kc == 0, kc == 15, r=[HT + ("a" if kc < 8 else "b"), "w1" + ("a" if kc < 8 else "b")], w=[f"pKV{pk}"])
                        for h in range(4):
                            act(junk[:R, h * 128:(h + 1) * 128], pKV[pk][:R, h * 128:(h + 1) * 128], AF.Square,
                                r=[f"pKV{pk}"], w=["junk", f"ssk{i2}"], accum_out=ssk[i2][:R, h:h + 1])
                        rstd_ops(ssk[i2][:R, :4], lnk[i2][:R, :4], rk[i2][:R, :4], 1.0 / 128, [f"ssk{i2}", f"lnk{i2}", f"rk{i2}"])
                        kdst = k32[i2] if own else kbf[i2]
                        kkey = f"k32{i2}" if own else f"kbf{i2}"
                        for h in range(4):
                            stt(kdst[:R, h * 128:(h + 1) * 128], pKV[pk][:R, h * 128:(h + 1) * 128], rk[i2][:R, h:h + 1],
                                gkb[:R, :], ALU.mult, ALU.mult, r=[f"pKV{pk}", f"rk{i2}"], w=[kkey])
                        if own:
                            cp('dve', kbf[i2][:R, :], k32[i2][:R, :], r=[kkey], w=[f"kbf{i2}"])
                            if om < 8:
                                dma('sp', d["k_p"][om * 128:(om + 1) * 128, hg * 512:(hg + 1) * 512], k32[i2][:, :], f"ko{i2}", r=[kkey])
                            else:
                                dma('sp', d["k_s"][:, hg * 512:(hg + 1) * 512], k32[i2][:R, :], f"ko{i2}", r=[kkey])
                        pv = rot % 4; rot += 1
                        for kc in range(16):
                            mm(pKV[pv][:R, :], hnT[i2][:, kc, :R], wv[:, kc, :], kc == 0, kc == 15, r=[HT + ("a" if kc < 8 else "b"), "w2" + ("a" if kc < 8 else "b")], w=[f"pKV{pv}"])
                        if own:
                            cp('act', v32[i2][:R, :], pKV[pv][:R, :], r=[f"pKV{pv}"], w=[f"v32{i2}"])
                            if om < 8:
                                cp('dve', Vc[:, t, :], v32[i2][:, :], r=[f"v32{i2}"], w=[f"V{t}"])
                                dma('sp', d["v_p"][om * 128:(om + 1) * 128, hg * 512:(hg + 1) * 512], v32[i2][:, :], f"vo{i2}", r=[f"v32{i2}"])
                            else:
                                cp('dve', vnew[:R, hg * 512:(hg + 1) * 512], v32[i2][:R, :], r=[f"v32{i2}"], w=["vnew"])
                                dma('sp', d["v_s"][:, hg * 512:(hg + 1) * 512], v32[i2][:R, :], f"vo{i2}", r=[f"v32{i2}"])
                        else:
                            cp('act', Vc[:, t, :], pKV[pv][:, :], r=[f"pKV{pv}"], w=[f"V{t}"])
                        if own:
                            pq = rot % 4; rot += 1
                            for kc in range(16):
                                mm(pKV[pq][:R, :], hnT[i2][:, kc, :R], wq[:, kc, :], kc == 0, kc == 15, r=[HT + ("a" if kc < 8 else "b"), "w0" + ("a" if kc < 8 else "b")], w=[f"pKV{pq}"])
                            for h in range(4):
                                act(junk[:R, h * 128:(h + 1) * 128], pKV[pq][:R, h * 128:(h + 1) * 128], AF.Square,
                                    r=[f"pKV{pq}"], w=["junk", f"ssq{i2}"], accum_out=ssk[i2][:R, 4 + h:5 + h])
                            rstd_ops(ssk[i2][:R, 4:8], lnk[i2][:R, 4:8], rk[i2][:R, 4:8], 1.0 / 128, [f"ssq{i2}", f"lnq{i2}", f"rq{i2}"])
                            for h in range(4):
                                stt(q32[:R, h * 128:(h + 1) * 128], pKV[pq][:R, h * 128:(h + 1) * 128], rk[i2][:R, 4 + h:5 + h],
                                    gqs[:R, :], ALU.mult, ALU.mult, r=[f"pKV{pq}", f"rq{i2}"], w=["q32"])
                            cp('dve', qbf[:R, :], q32[:R, :], r=["q32"], w=["qbf"])
                        if mid is not None:
                            mid()
                        for h in range(4):
                            tr(pKT[:, h, :R], kbf[i2][:R, h * 128:(h + 1) * 128], r=[f"kbf{i2}", "ident"], w=["pKT"])
                        if t < 32:
                            cp('act', kT[:, :, t * 128:(t + 1) * 128], pKT[:, 0:4, :], r=["pKT"], w=[f"kT{t}"])
                        else:
                            cp('act', knew[:, hg * 4:(hg + 1) * 4, :R], pKT[:, 0:4, :R], r=["pKT"], w=["knew"])
                        if own:
                            for h in range(4):
                                tr(pKT[:, 4 + h, :R], qbf[:R, h * 128:(h + 1) * 128], r=["qbf", "ident"], w=["pKT"])
                            cp('act', qT[:, :, om * 128: om * 128 + R], pKT[:, 4:8, :R], r=["pKT"], w=[f"qT{om}"])
                        rotc[0] = rot

                    load_x(0)
                    load_x(1)
                    FE(0)
                    FE2(0)
                    for bi in range(len(blocks)):
                        if bi + 2 < len(blocks):
                            load_x(bi + 2)
                        if bi + 1 < len(blocks):
                            FE(bi + 1)
                            BE(bi, mid=lambda b=bi + 1: FE2(b))
                        else:
                            BE(bi)
                    if 'A' in skip:
                        S.ops = []
                    S.flush(blk)
                if stop_after == 'A' and hg == 0:
                    return nc

                with contextlib.ExitStack() as esB:
                    kTs = sb("kTs", [128, 4, 4096], BF, esB)
                    Vs = sb("Vs", [128, 32, 512], BF, esB)
                    with contextlib.ExitStack() as esB1, nc.Block() as blk:
                        e32 = [sb(f"e32{i}", [128, 512], F32, esB1) for i in range(2)]
                        spb = [sb(f"spb{i}", [128, 512], BF, esB1) for i in range(2)]
                        acc32 = [sb(f"acc32{i}", [128, 512], F32, esB1) for i in range(2)]
                        accbf = [sb(f"accbf{i}", [128, 512], BF, esB1) for i in range(3)]
                        wtb = [sb(f"wtb{i}", [128, 512], BF, esB1) for i in range(2)]
                        osq = sb("osq", [128, 512], F32, esB1)
                        osqh = sb("osqh", [128, 512], BF, esB1); osql = sb("osql", [128, 512], BF, esB1)
                        pS = [ps(f"pS{i}", [128, 512], F32, esB1) for i in range(2)]
                        pW = [ps(f"pW{i}", [128, 512], F32, esB1) for i in range(2)]
                        pO = [ps(f"pO{i}", [128, 512], F32, esB1) for i in range(2)]
                        pM = ps("pM", [128, 512], F32, esB1)
                        kst = [sb(f"kst{i}", [128, 4, 512], BF, esB1) for i in range(2)]
                        pT2 = ps("pT2", [128, 2, 512], BF, esB1)
                        ckv = d["ck"].rearrange("(j p) n -> p j n", p=128)
                        cvv = d["cv"].rearrange("(j p) n -> p j n", p=128)
                        def load_vs():
                            for q4 in range(4):
                                dma('pool', Vs[:, q4 * 8:(q4 + 1) * 8, :], cvv[:, q4 * 8:(q4 + 1) * 8, hg * 512:(hg + 1) * 512], 'vs', w=["Vs"])

                        def load_k(g):
                            dma('pool', kst[g % 2][:, :, :], ckv[:, g * 4:(g + 1) * 4, hg * 512:(hg + 1) * 512], f"kst{g % 2}", w=[f"kst{g % 2}"])

                        load_k(0)
                        b0_tasks = []
                        for g in range(8):
                            for half in range(2):
                                def task(g=g, half=half):
                                    i2 = g % 2
                                    if half == 1 and g + 1 < 8:
                                        load_k(g + 1)
                                    if g == 0 and half == 1:
                                        load_vs()
                                    for j in range(4):
                                        for hh in range(2):
                                            h = 2 * half + hh
                                            tr(pT2[:, hh, j * 128:(j + 1) * 128], kst[i2][:, j, h * 128:(h + 1) * 128],
                                               r=[f"kst{i2}", "ident"], w=["pT2"])
                                    cp('dve', kTs[:, 2 * half:2 * half + 2, g * 512:(g + 1) * 512], pT2[:, :, :], r=["pT2"], w=["kTs"])
                                b0_tasks.append(task)

                        units = []
                        seqs = []
                        for s in range(2):
                            top = 4 * s * 4 + 15 if s == 1 else 15
                            top = 15 if s == 0 else 31
                            for h in range(4):
                                sq = len(seqs)
                                seqs.append(dict(kind='p', s=s, h=h))
                                for j in range(top, -1, -1):
                                    i = j - (top - 3)
                                    if i == 3:
                                        c0, mask = 0, (mask3b[:, :], 0, 512)
                                    elif i >= 0:
                                        c0, mask = 128 * i, (masktrib[:, :], 0, 128)
                                    else:
                                        c0, mask = 0, None
                                    n = 512 - c0
                                    units.append(dict(seq=sq, first=(j == top), last=(j == 0), c0=c0, n=n, mask=mask,
                                                      heads=[(kT[:, h, j * 128:(j + 1) * 128], qT[:, h, s * 512 + c0: s * 512 + 512],
                                                              Vc[:, j, h * 128:(h + 1) * 128], 0, n)]))
                        sq = len(seqs)
                        seqs.append(dict(kind='s'))
                        for j in range(32, -1, -1):
                            if j == 32:
                                heads = [(knew[:, hg * 4 + h, :], qT[:, h, 1024:1088], vnew[:, (hg * 4 + h) * 128:(hg * 4 + h + 1) * 128], h * 64, 64)
                                         for h in range(4)]
                                mask = (masks4b[:, :], 0, 256)
                            else:
                                heads = [(kTs[:, h, j * 128:(j + 1) * 128], qT[:, h, 1024:1088], Vs[:, j, h * 128:(h + 1) * 128], h * 64, 64)
                                         for h in range(4)]
                                mask = None
                            units.append(dict(seq=sq, first=(j == 32), last=(j == 0), c0=0, n=256, mask=mask, heads=heads))

                        seqpos = {}

                        def stage1(ui):
                            u = units[ui]; b = ui % 2
                            c0, n = u['c0'], u['n']
                            nh = len(u['heads'])
                            xr = ["kTs"] if seqs[u['seq']]['kind'] == 's' else []
                            for i, (ka, qa, va, col, hn_) in enumerate(u['heads']):
                                mm(pS[b][:, c0 + col:c0 + col + hn_], ka, qa, i == 0, (i == nh - 1) and u['mask'] is None,
                                   r=xr, w=[f"pS{b}"])
                            if u['mask'] is not None:
                                ma, mc, mn = u['mask']
                                mm(pS[b][:, c0 + mc:c0 + mc + mn], identb[:, :], ma, False, True, w=[f"pS{b}"])
                            act(e32[b][:, c0:c0 + n], pS[b][:, c0:c0 + n], AF.Exp, r=[f"pS{b}"], w=[f"e32{b}"])
                            act(spb[b][:, c0:c0 + n], e32[b][:, c0:c0 + n], AF.Ln, r=[f"e32{b}"], w=[f"spb{b}"], bias=1.0)
                            a = u['seq'] % 2
                            pos = seqpos.get(u['seq'], 0)
                            u['pos'] = pos
                            seqpos[u['seq']] = pos + 1
                            if not u['last']:
                                if u['first']:
                                    cp('dve', acc32[a][:, c0:c0 + n], spb[b][:, c0:c0 + n], r=[f"spb{b}"], w=[f"acc32{a}"])
                                else:
                                    tt('dve', acc32[a][:, c0:c0 + n], acc32[a][:, c0:c0 + n], spb[b][:, c0:c0 + n], ALU.add,
                                       r=[f"spb{b}", f"acc32{a}"], w=[f"acc32{a}"])
                                W = 512 if seqs[u['seq']]['kind'] == 'p' else 256
                                cp("dve", accbf[ui % 3][:, :W], acc32[a][:, :W], r=[f"acc32{a}"], w=[f"accbf{ui % 3}"])

                        def stage2(ui):
                            u = units[ui]; b = ui % 2
                            c0, n = u['c0'], u['n']
                            xr = ["kTs"] if seqs[u['seq']]['kind'] == 's' else []
                            for i, (ka, qa, va, col, hn_) in enumerate(u['heads']):
                                mm(pW[b][:, c0 + col:c0 + col + hn_], ka, qa, i == 0, False, r=xr, w=[f"pW{b}"])
                            if u['mask'] is not None:
                                ma, mc, mn = u['mask']
                                mm(pW[b][:, c0 + mc:c0 + mc + mn], identb[:, :], ma, False, False, w=[f"pW{b}"])
                            mm(pW[b][:, c0:c0 + n], negTb[:, :], spb[b][:, c0:c0 + n], False, u['first'], r=[f"spb{b}"], w=[f"pW{b}"])
                            if not u['first']:
                                pp = (ui - 1) % 3
                                mm(pW[b][:, c0:c0 + n], negonesb[:, :], accbf[pp][:, c0:c0 + n], False, True,
                                   r=[f"accbf{pp}"], w=[f"pW{b}"])
                            act(wtb[b][:, c0:c0 + n], pW[b][:, c0:c0 + n], AF.Exp, r=[f"pW{b}"], w=[f"wtb{b}"])

                        def stage3(ui):
                            u = units[ui]; b = ui % 2
                            c0, n = u['c0'], u['n']
                            o = u['seq'] % 2
                            nh = len(u['heads'])
                            for i, (ka, qa, va, col, hn_) in enumerate(u['heads']):
                                mm(pO[o][:, c0 + col:c0 + col + hn_], va, wtb[b][:, c0 + col:c0 + col + hn_],
                                   u['first'] and i == 0, u['last'] and i == nh - 1,
                                   r=[f"wtb{b}"] + (["Vs"] if seqs[u['seq']]['kind'] == 's' else []), w=[f"pO{o}"])
                            if u['last'] and 'e' not in _os.environ.get("BOFF", ""):
                                sq_ = seqs[u['seq']]
                                first_acc = (hg == 0 and sq_.get('h', 0) == 0)
                                if sq_['kind'] == 'p':
                                    s, h = sq_['s'], sq_['h']
                                    hgl = hg * 4 + h
                                    act(osq[:, :], pO[o][:, :], AF.Square, r=[f"pO{o}"], w=["osq"])
                                    ts('dve', mixA[:, hgl, s * 512:(s + 1) * 512], pO[o][:, :], gaT[:, hgl:hgl + 1], ALU.mult,
                                       r=[f"pO{o}"], w=[f"mixA{hgl}_{s}"])
                                    cp("dve", osqh[:, :], osq[:, :], r=["osq"], w=["osqh"])
                                    tt('dve', osql[:, :], osq[:, :], osqh[:, :], ALU.subtract, r=["osq", "osqh"], w=["osql"])
                                    for m4 in range(4):
                                        mm(pM[:, m4 * 128:(m4 + 1) * 128], osqh[:, m4 * 128:(m4 + 1) * 128], onesm[:, :], m4 == 0, False, r=["osqh"], w=["pM"])
                                    for m4 in range(4):
                                        mm(pM[:, m4 * 128:(m4 + 1) * 128], osql[:, m4 * 128:(m4 + 1) * 128], onesm[:, :], False, m4 == 3, r=["osql"], w=["pM"])
                                    pMc = pM[:, :].rearrange("p (a b) -> p a b", b=128)[:, :, 0]
                                    if first_acc:
                                        cp('dve', ssq_a[:, 4 * s:4 * s + 4], pMc, r=["pM"], w=["ssq_a"])
                                    else:
                                        tt('dve', ssq_a[:, 4 * s:4 * s + 4], ssq_a[:, 4 * s:4 * s + 4], pMc, ALU.add, r=["pM", "ssq_a"], w=["ssq_a"])
                                else:
                                    act(osq[:, :256], pO[o][:, :256], AF.Square, r=[f"pO{o}"], w=["osq"])
                                    for h in range(4):
                                        hgl = hg * 4 + h
                                        ts('dve', mixA[:, hgl, 1024:1088], pO[o][:, h * 64:(h + 1) * 64], gaT[:, hgl:hgl + 1], ALU.mult,
                                           r=[f"pO{o}"], w=[f"mixA{hgl}_2"])
                                    cp("dve", osqh[:, :256], osq[:, :256], r=["osq"], w=["osqh"])
                                    tt('dve', osql[:, :256], osq[:, :256], osqh[:, :256], ALU.subtract, r=["osq", "osqh"], w=["osql"])
                                    for h in range(4):
                                        mm(pM[:64, 0:128], osqh[:, h * 64:(h + 1) * 64], onesm[:, :], h == 0, False, r=["osqh"], w=["pM"])
                                    for h in range(4):
                                        mm(pM[:64, 0:128], osql[:, h * 64:(h + 1) * 64], onesm[:, :], False, h == 3, r=["osql"], w=["pM"])
                                    if hg == 0:
                                        cp('dve', ssq_a[:64, 8:9], pM[:64, 0:1], r=["pM"], w=["ssq_a"])
                                    else:
                                        tt('dve', ssq_a[:64, 8:9], ssq_a[:64, 8:9], pM[:64, 0:1], ALU.add, r=["pM", "ssq_a"], w=["ssq_a"])

                        NU = len(units)
                        if _os.environ.get("BLIMIT"):
                            NU = int(_os.environ["BLIMIT"])
                        _off = _os.environ.get("BOFF", "")
                        for it in range(NU + 2):
                            if it % 8 == 4 and b0_tasks:
                                b0_tasks.pop(0)()
                            if it < NU:
                                stage1(it)
                            if 0 <= it - 1 < NU and '2' not in _off:
                                stage2(it - 1)
                            if 0 <= it - 2 < NU and '3' not in _off and '2' not in _off:
                                stage3(it - 2)
                        while b0_tasks:
                            b0_tasks.pop(0)()
                        S.flush(blk)
                if stop_after == 'B' and hg == 0:
                    return nc

        with contextlib.ExitStack() as esC:
            x1 = sb("x1", [128, 9, D], F32, esC)
            hTo = sb("hTo", [128, 16, NOWN], BF, esC)
            mblocks = [(m, 128) for m in range(8)] + [(8, 64)]
            tiles = [(0, 512), (512, 512), (1024, 64)]

            def xsrc(m):
                if m < 4:
                    return d["x_ctx"][(12 + m) * 128:(13 + m) * 128, :]
                if m < 8:
                    return d["x_ctx"][(24 + m) * 128:(25 + m) * 128, :]
                return d["x_smp"][:, :]

            def norm_T(esX, blk, gname, dstT, load, halo=None):
                gb = sb("gb_" + gname, [128, D], F32, esX)
                hn = [sb(f"hnc{i}", [128, D], BF, esX) for i in range(2)]
                ss = [sb(f"ssc{i}", [128, 1], F32, esX) for i in range(2)]
                lnv = [sb(f"lnvc{i}", [128, 1], F32, esX) for i in range(2)]
                rs = [sb(f"rsc{i}", [128, 1], F32, esX) for i in range(2)]
                pT = ps("pTc", [128, 16, 128], BF, esX)
                dma('sp', gb[:], d[gname][:, :], 'c0', w=["gb"])
                items = [(x1[:R, m, :], R, dstT[:, :, m * 128:m * 128 + R], f"x1_{m}", m) for m, R in mblocks]
                if halo is not None:
                    items.append(halo)
                if load:
                    for idx, (xa, R, dst, xkey, m) in enumerate(items):
                        if m is not None:
                            dma('sp', xa, xsrc(m), f"x1l{idx}", w=[xkey])

                def fe1(idx):
                    xa, R, dst, xkey, m = items[idx]
                    i2 = idx % 2
                    act(hn[i2][:R, :], xa, AF.Square, r=[xkey], w=[f"hnc{i2}", f"ssc{i2}"], accum_out=ss[i2][:R, :])
                    rstd_ops(ss[i2][:R, :], lnv[i2][:R, :], rs[i2][:R, :], 1.0 / D, [f"ssc{i2}", f"lnvc{i2}", f"rsc{i2}"])
                    stt(hn[i2][:R, :], xa, rs[i2][:R, :], gb[:R, :], ALU.mult, ALU.mult, r=[xkey, f"rsc{i2}", "gb"], w=[f"hnc{i2}"])

                fe1(0)
                for idx, (xa, R, dst, xkey, m) in enumerate(items):
                    i2 = idx % 2
                    if idx + 1 < len(items):
                        fe1(idx + 1)
                    for kc in range(16):
                        tr(pT[:, kc, :R], hn[i2][:R, kc * 128:(kc + 1) * 128], r=[f"hnc{i2}", "ident"], w=["pTc"])
                    cp('act', dst, pT[:, :, :R], r=["pTc"], w=["dstT"])

            with contextlib.ExitStack() as esC12:
                hTh = sb("hTh", [128, 16, 4], BF, esC12)
                mixC = sb("mixC", [128, 8, NOWN], BF, esC12)
                with contextlib.ExitStack() as esX, nc.Block() as blk:
                    xh = sb("xh", [4, D], F32, esX)
                    dma('sp', xh[0:2, :], d["x_ctx"][1534:1536, :], 'xh', w=["xh"])
                    dma('sp', xh[2:4, :], d["x_ctx"][3582:3584, :], 'xh', w=["xh"])
                    norm_T(esX, blk, "g1bc", hTo, True, halo=(xh[:, :], 4, hTh[:, :, :], "xh", None))
                    S.flush(blk)
                with contextlib.ExitStack() as esX, nc.Block() as blk:
                    wc = [sb(f"wc{i}", [128, 3, 16, 128], BF, esX) for i in range(2)]
                    wst = [sb(f"wst{i}", [128, 16, 128], F32, esX) for i in range(2)]
                    u32 = [sb(f"u32{i}", [128, 516], F32, esX) for i in range(2)]
                    hc32 = [sb(f"hc32{i}", [128, 512], F32, esX) for i in range(1)] * 2
                    t1 = [sb(f"t1{i}", [128, 512], F32, esX) for i in range(2)]
                    oc32 = [sb(f"oc32{i}", [128, 512], F32, esX) for i in range(2)]
                    osqc = [sb(f"osqc{i}", [128, 512], F32, esX) for i in range(1)] * 2
                    osqch = [sb(f"osqch{i}", [128, 512], BF, esX) for i in range(1)] * 2
                    osqcl = [sb(f"osqcl{i}", [128, 512], BF, esX) for i in range(1)] * 2
                    hh = sb("hh", [128, 4], F32, esX); uh = sb("uh", [128, 4], F32, esX)
                    cpT = sb("cpT", [128, 16], F32, esX); csT = sb("csT", [128, 16], F32, esX)
                    pG = [[ps(f"pG{i}{j}", [128, 512], F32, esX) for j in range(3)] for i in range(2)]
                    pH = ps("pH", [128, 8], F32, esX)
                    pQ = ps("pQ", [128, 16], F32, esX)
                    cnt = 0
                    pend = [None]
                    def stage_w(g):
                        wi = g % 2
                        for j3, cbase in enumerate((3072, 4096, 5120)):
                            si = (3 * g + j3) % 2
                            dma('sp', wst[si][:, :, :], w_in_v[:, :, cbase + g * 128: cbase + (g + 1) * 128], f"wst{si}", w=[f"wst{si}"])
                            cp('act', wc[wi][:, j3, :, :], wst[si][:, :, :], r=[f"wst{si}"], w=[f"wc{wi}"])

                    stage_w(0)
                    for g in range(8):
                        wi = g % 2
                        if g + 1 < 8:
                            stage_w(g + 1)
                        for j3 in (1, 2):
                            for kc in range(16):
                                mm(pH[:, (j3 - 1) * 4:(j3 - 1) * 4 + 4], wc[wi][:, j3, kc, :], hTh[:, kc, :], kc == 0 and j3 == 1, kc == 15 and j3 == 2,
                                   r=[f"wc{wi}"], w=["pH"])
                        cp('act', hh[:, :], pH[:, 4:8], r=["pH"], w=["hh"])
                        tt('dve', uh[:, :], pH[:, 0:4], hh[:, :], ALU.mult, r=["pH", "hh"], w=["uh"])
                        for ti, (T0, n) in enumerate(tiles):
                            b = cnt % 2; cnt += 1
                            for j3 in range(3):
                                for kc in range(16):
                                    mm(pG[b][j3][:, :n], wc[wi][:, j3, kc, :], hTo[:, kc, T0:T0 + n], kc == 0, kc == 15, r=[f"wc{wi}"], w=[f"pG{b}{j3}"])
                            if pend[0] is not None:
                                pend[0]()
                                pend[0] = None
                            U = f"u32{b}"
                            cp('act', hc32[b][:, :n], pG[b][2][:, :n], r=[f"pG{b}2"], w=["hc32"])
                            tt('dve', u32[b][:, 2:2 + n], pG[b][1][:, :n], hc32[b][:, :n], ALU.mult, r=[f"pG{b}1", "hc32"], w=[U])
                            if ti < 2:
                                cp('dve', u32[b][:, 0:2], uh[:, 2 * ti:2 * ti + 2], r=["uh"], w=[U])
                            else:
                                cp('dve', u32[b][:, 0:2], scT[:, 2 * g:2 * g + 2], w=[U])
                            ts('dve', t1[b][:, :n], u32[b][:, 2:2 + n], cwT[:, 3 * g + 2:3 * g + 3], ALU.mult, r=[U], w=[f"t1{b}"])
                            stt(t1[b][:, :n], u32[b][:, 1:1 + n], cwT[:, 3 * g + 1:3 * g + 2], t1[b][:, :n], ALU.mult, ALU.add, r=[U, f"t1{b}"], w=[f"t1{b}"])
                            stt(t1[b][:, :n], u32[b][:, 0:n], cwT[:, 3 * g:3 * g + 1], t1[b][:, :n], ALU.mult, ALU.add, r=[U, f"t1{b}"], w=[f"t1{b}"])
                            tt('dve', oc32[b][:, :n], pG[b][0][:, :n], t1[b][:, :n], ALU.mult, r=[f"pG{b}0", f"t1{b}"], w=[f"oc32{b}"])
                            ts('dve', mixC[:, g, T0:T0 + n], oc32[b][:, :n], gcT[:, g:g + 1], ALU.mult, r=[f"oc32{b}"], w=["mixC"])
                            act(osqc[b][:, :n], oc32[b][:, :n], AF.Square, r=[f"oc32{b}"], w=["osqc"])
                            if ti == 1:
                                cp('pool', cpT[:, 2 * g:2 * g + 2], u32[b][:, 512:514], r=[U], w=["cpT"])
                            if ti == 2:
                                cp('pool', csT[:, 2 * g:2 * g + 2], u32[b][:, 64:66], r=[U], w=["csT"])
                            nw = 128 * ((n + 127) // 128)
                            cp('act', osqch[b][:, :nw], osqc[b][:, :nw], r=["osqc"], w=["osqch"])
                            tt('dve', osqcl[b][:, :nw], osqc[b][:, :nw], osqch[b][:, :nw], ALU.subtract, r=["osqc", "osqch"], w=["osqcl"])
                            def ssq_mms(g=g, n=n, T0=T0, b=b):
                                for m4 in range((n + 127) // 128):
                                    R = 128
                                    m = T0 // 128 + m4
                                    mm(pQ[:R, m:m + 1], osqch[b][:, m4 * 128:m4 * 128 + R], onesb[:, :], g == 0 and m == 0, False,
                                       r=["osqch"], w=["pQ"])
                                    mm(pQ[:R, m:m + 1], osqcl[b][:, m4 * 128:m4 * 128 + R], onesb[:, :], False, g == 7 and m == 8,
                                       r=["osqcl"], w=["pQ"])
                            pend[0] = ssq_mms
                    if pend[0] is not None:
                        pend[0]()
                    cp('dve', ssq_c[:, :], pQ[:, 0:9], r=["pQ"], w=["ssq_c"])
                    dma('sp', d["conv_pT"][:, :], cpT[:, :], 'cvo', r=["cpT"])
                    dma('sp', d["conv_sT"][:, :], csT[:, :], 'cvo', r=["csT"])
                    S.flush(blk)
                if stop_after == 'C1':
                    return nc
                with contextlib.ExitStack() as esX, nc.Block() as blk:
                    wo = [sb(f"wo{i}", [128, 16, 512], BF, esX) for i in range(2)]
                    pA = [ps(f"pA{i}", [128, 512], F32, esX) for i in range(2)]
                    pC = [ps(f"pC{i}", [128, 512], F32, esX) for i in range(2)]
                    g2b = sb("g2b", [128, D], F32, esX)
                    hn2 = [sb(f"hn2{i}", [128, D], BF, esX) for i in range(2)]
                    ss2 = [sb(f"ss2{i}", [128, 1], F32, esX) for i in range(2)]
                    ln2 = [sb(f"ln2{i}", [128, 1], F32, esX) for i in range(2)]
                    rs2 = [sb(f"rs2{i}", [128, 1], F32, esX) for i in range(2)]
                    pT2n = ps("pT2n", [128, 16, 128], BF, esX)
                    dma('sp', g2b[:], d["g2bc"][:, :], 'c0', w=["g2b"])

                    def n2_fe1(m, R):
                        i2 = m % 2
                        xk = [f"x1_{m}_{k}" for k in range(4)]
                        act(hn2[i2][:R, :], x1[:R, m, :], AF.Square, r=xk, w=[f"hn2{i2}", f"ss2{i2}"], accum_out=ss2[i2][:R, :])
                        rstd_ops(ss2[i2][:R, :], ln2[i2][:R, :], rs2[i2][:R, :], 1.0 / D, [f"ss2{i2}", f"ln2{i2}", f"rs2{i2}"])
                        stt(hn2[i2][:R, :], x1[:R, m, :], rs2[i2][:R, :], g2b[:R, :], ALU.mult, ALU.mult, r=xk + [f"rs2{i2}", "g2b"], w=[f"hn2{i2}"])

                    def n2_fe2(m, R):
                        i2 = m % 2
                        for kc in range(16):
                            tr(pT2n[:, kc, :R], hn2[i2][:R, kc * 128:(kc + 1) * 128], r=[f"hn2{i2}", "ident"], w=["pT2n"])
                        cp('act', hTo[:, :, m * 128:m * 128 + R], pT2n[:, :, :R], r=["pT2n"], w=["h2T"])
                    S.op('pool', lambda e: e.memset(ssq_a[64:128, 8:9], 1.0), w=["ssq_a"])
                    S.op('pool', lambda e: e.memset(ssq_c[64:128, 8:9], 1.0), w=["ssq_c"])
                    rstd_ops(ssq_a[:, :], lna[:, :], ra[:, :], 1.0 / 1024, ["ssq_a", "lna", "ra"])
                    rstd_ops(ssq_c[:, :], lnc[:, :], rc[:, :], 1.0 / 1024, ["ssq_c", "lnc", "rc"])
                    w_out_v = d["w_out"].rearrange("(kc p) n -> p kc n", p=128)
                    cnt = 0
                    for nn in range(4):
                        wi = nn % 2
                        for half in range(2):
                            dma('pool', wo[wi][:, half * 8:(half + 1) * 8, :], w_out_v[:, half * 8:(half + 1) * 8, nn * 512:(nn + 1) * 512], f"wo{wi}", w=[f"wo{wi}"])
                        for m, R in mblocks:
                            b = cnt % 2; cnt += 1
                            for kc in range(8):
                                mm(pA[b][:R, :], mixA[:, kc, m * 128:m * 128 + R], wo[wi][:, kc, :], kc == 0, kc == 7, r=[f"wo{wi}"], w=[f"pA{b}"])
                            for kc in range(8):
                                mm(pC[b][:R, :], mixC[:, kc, m * 128:m * 128 + R], wo[wi][:, 8 + kc, :], kc == 0, kc == 7, r=[f"wo{wi}"], w=[f"pC{b}"])
                            xs = x1[:R, m, nn * 512:(nn + 1) * 512]
                            stt(xs, pA[b][:R, :], ra[:R, m:m + 1], xs, ALU.mult, ALU.add, r=[f"pA{b}", "ra", f"x1_{m}_{nn}"], w=[f"x1_{m}_{nn}"])
                            stt(xs, pC[b][:R, :], rc[:R, m:m + 1], xs, ALU.mult, ALU.add, r=[f"pC{b}", "rc", f"x1_{m}_{nn}"], w=[f"x1_{m}_{nn}"])
                            if nn == 3 and stop_after != 'C2':
                                n2_fe1(m, R)
                                if m >= 1:
                                    n2_fe2(m - 1, mblocks[m - 1][1])
                    if stop_after != 'C2':
                        n2_fe2(8, 64)
                    S.flush(blk)
                if stop_after == 'C2':
                    with nc.Block() as blk:
                        for m, R in mblocks:
                            if m < 8:
                                dma('sp', d["y_p"][m * 128:(m + 1) * 128, :], x1[:, m, :], 'yo')
                            else:
                                dma('sp', d["y_s"][:, :], x1[:64, m, :], 'yo')
                        dma('sp', d["dbgA"][:, :], mixA[:, :, :].rearrange("p h t -> p (h t)"), 'yo')
                        dma('sp', d["dbgC"][:, :], mixC[:, :, :].rearrange("p h t -> p (h t)"), 'yo')
                        dma('sp', d["dbgS"][:, 0:9], ssq_a[:, :], 'yo')
                        dma('sp', d["dbgS"][:, 9:18], ssq_c[:, :], 'yo')
                        S.flush(blk)
            if stop_after == 'C2':
                return nc
            with contextlib.ExitStack() as esC34:
                h2T = hTo
                with contextlib.ExitStack() as esX, nc.Block() as blk:
                    GW = 256
                    NG = DFF // GW
                    wg = [sb(f"wg{i}", [128, 16, GW], BF, esX) for i in range(2)]
                    wu = [sb(f"wu{i}", [128, 16, GW], BF, esX) for i in range(2)]
                    wd = [sb(f"wd{i}", [128, 2, D], BF, esX) for i in range(2)]
                    aT = [sb(f"aT{i}", [128, 2, NOWN], BF, esX) for i in range(2)]
                    sg = [sb(f"sg{i}", [128, 512], F32, esX) for i in range(2)]
                    pGt = [ps(f"pGt{i}", [128, 512], F32, esX) for i in range(2)]
                    pUt = [ps(f"pUt{i}", [128, 512], F32, esX) for i in range(2)]
                    pD = [ps(f"pD{i}", [128, 512], F32, esX) for i in range(4)]
                    wgv = d["w_gate"].rearrange("(kc p) n -> p kc n", p=128)
                    wuv = d["w_up"].rearrange("(kc p) n -> p kc n", p=128)
                    wdv = d["w_down"].rearrange("(f p) n -> p f n", p=128)
                    cg = [0]; cd = [0]

                    def load_w(G):
                        wi = G % 2
                        dma('pool', wg[wi][:, :, :], wgv[:, :, G * GW:(G + 1) * GW], f"wg{wi}", w=[f"wg{wi}"])
                        dma('pool', wu[wi][:, :, :], wuv[:, :, G * GW:(G + 1) * GW], f"wu{wi}", w=[f"wu{wi}"])

                    def load_wd(G):
                        wi = G % 2
                        dma('pool', wd[wi][:, :, :], wdv[:, 2 * G:2 * G + 2, :], f"wd{wi}", w=[f"wd{wi}"])

                    def gateup(G, f, T0, n):
                        wi = G % 2
                        b = cg[0] % 2; cg[0] += 1
                        for kc in range(16):
                            mm(pGt[b][:, :n], wg[wi][:, kc, f * 128:(f + 1) * 128], h2T[:, kc, T0:T0 + n], kc == 0, kc == 15, r=[f"wg{wi}"], w=[f"pGt{b}"])
                        for kc in range(16):
                            mm(pUt[b][:, :n], wu[wi][:, kc, f * 128:(f + 1) * 128], h2T[:, kc, T0:T0 + n], kc == 0, kc == 15, r=[f"wu{wi}"], w=[f"pUt{b}"])
                        act(sg[b][:, :n], pGt[b][:, :n], AF.Silu, r=[f"pGt{b}"], w=[f"sg{b}"])
                        tt('dve', aT[wi][:, f, T0:T0 + n], pUt[b][:, :n], sg[b][:, :n], ALU.mult, r=[f"pUt{b}", f"sg{b}"], w=[f"aT{wi}"])

                    def down(G, m, R, nn):
                        wi = G % 2
                        b = cd[0] % 4; cd[0] += 1
                        for f in range(2):
                            mm(pD[b][:R, :], aT[wi][:, f, m * 128:m * 128 + R], wd[wi][:, f, nn * 512:(nn + 1) * 512], f == 0, f == 1,
                               r=[f"aT{wi}", f"wd{wi}"], w=[f"pD{b}"])
                        xs = x1[:R, m, nn * 512:(nn + 1) * 512]
                        tt('dve', xs, xs, pD[b][:R, :], ALU.add, r=[f"pD{b}", f"x1_{m}_{nn}"], w=[f"x1_{m}_{nn}"])
                        if G == NG - 1 and nn == 3:
                            if m < 8:
                                dma('sp', d["y_p"][m * 128:(m + 1) * 128, :], x1[:, m, :], 'yo', r=[f"x1_{m}_{k}" for k in range(4)])
                            else:
                                dma('sp', d["y_s"][:, :], x1[:64, m, :], 'yo', r=[f"x1_{m}_{k}" for k in range(4)])

                    load_w(0)
                    load_wd(0)
                    for G in range(NG + 1):
                        if G + 1 < NG:
                            load_w(G + 1)
                        gu = [(f, T0, n) for f in range(2) for (T0, n) in tiles] if G < NG else []
                        dn = [(m, R, nn) for (m, R) in mblocks for nn in range(4)] if G >= 1 else []
                        per = (len(dn) + 5) // 6
                        for i in range(6):
                            if i < len(gu):
                                gateup(G, *gu[i])
                            for (m, R, nn) in dn[i * per:(i + 1) * per]:
                                down(G - 1, m, R, nn)
                        if G + 1 < NG:
                            load_wd(G + 1)
                    S.flush(blk)
    return nc


_NC_CACHE = {}


def _consts():
    bf = ml_dtypes.bfloat16
    k = np.arange(128)[:, None]
    q = np.arange(128)[None, :]
    ident = np.eye(128, dtype=np.float32).astype(bf)
    negT = np.where(k >= q, -1.0, 0.0).astype(np.float32).astype(bf)
    negones = np.full((128, 128), -1.0, np.float32).astype(bf)
    masktri = np.where(k >= q, NEG, 0.0).astype(np.float32)
    mask3 = np.full((128, 512), NEG, np.float32)
    mask3[:, 384:] = masktri
    q64 = np.arange(64)[None, :]
    ms = np.where((k >= q64) | (k >= 64), NEG, 0.0).astype(np.float32)
    masks4 = np.tile(ms, (1, 4))
    return dict(ident=ident, negT=negT, negones=negones, masktri=masktri.astype(bf), mask3=mask3.astype(bf),
                masks4=masks4.astype(bf), ones32=np.ones((128, 1), np.float32), onesb=np.ones((128, 1), np.float32).astype(bf),
                onesm=np.ones((128, 128), np.float32).astype(bf))


def kernel(x_prompt, x_sample, cache_k, cache_v, state_conv, g_norm1, w_in, g_q, g_k, conv_w,
           g_attn_out, g_conv_out, w_out, g_norm2, w_gate, w_up, w_down):
    f = lambda a: np.ascontiguousarray(np.asarray(a, dtype=np.float32))
    x_prompt, x_sample, cache_k, cache_v, state_conv = map(f, (x_prompt, x_sample, cache_k, cache_v, state_conv))
    if 'nc' not in _NC_CACHE:
        _NC_CACHE['nc'] = build()
    nc = _NC_CACHE['nc']
    cst = _consts()
    shared = dict(
        w_in=f(w_in)[0], w_out=f(w_out)[0], w_gate=f(w_gate)[0], w_up=f(w_up)[0], w_down=f(w_down)[0],
        g1bc=np.ascontiguousarray(np.broadcast_to(f(g_norm1)[0][None, :], (128, D))),
        g2bc=np.ascontiguousarray(np.broadcast_to(f(g_norm2)[0][None, :], (128, D))),
        gqbc=np.ascontiguousarray(np.broadcast_to(f(g_q)[0][None, :], (128, 128))),
        gkbc=np.ascontiguousarray(np.broadcast_to(f(g_k)[0][None, :], (128, 128))),
        gaT=np.ascontiguousarray(f(g_attn_out)[0].reshape(8, 128).T),
        gcT=np.ascontiguousarray(f(g_conv_out)[0].reshape(8, 128).T),
        cwT=np.ascontiguousarray(f(conv_w)[0].reshape(3, 8, 128).transpose(2, 1, 0).reshape(128, 24)),
        **cst)
    in_maps = []
    for c in range(8):
        b, r = c // 4, c % 4
        xc = np.zeros((4096, D), np.float32)
        xc[(3 - r) * 512:] = x_prompt[b, :(5 + r) * 512]
        m = dict(shared)
        m.update(x_ctx=xc, x_smp=x_sample[c], ck=cache_k[0, c].reshape(4096, 1024), cv=cache_v[0, c].reshape(4096, 1024),
                 scT=np.ascontiguousarray(state_conv[0, c].reshape(2, 8, 128).transpose(2, 1, 0).reshape(128, 16)))
        in_maps.append(m)
    res = run_bass_kernel_spmd(nc, in_maps, core_ids=list(range(8)))
    R = res.results
    y_p = np.zeros((2, 4096, D), np.float32); k_p = np.zeros((1, 2, 4096, 8, 128), np.float32); v_p = np.zeros_like(k_p)
    y_s = np.zeros((8, 64, D), np.float32); k_s = np.zeros((1, 8, 64, 8, 128), np.float32); v_s = np.zeros_like(k_s)
    c_p = np.zeros((1, 2, 2, 1024), np.float32); c_s = np.zeros((1, 8, 2, 1024), np.float32)
    for c in range(8):
        b, r = c // 4, c % 4
        o = R[c]
        for s, ch in enumerate((r, 4 + r)):
            y_p[b, ch * 512:(ch + 1) * 512] = o["y_p"][s * 512:(s + 1) * 512]
            k_p[0, b, ch * 512:(ch + 1) * 512] = o["k_p"][s * 512:(s + 1) * 512].reshape(512, 8, 128)
            v_p[0, b, ch * 512:(ch + 1) * 512] = o["v_p"][s * 512:(s + 1) * 512].reshape(512, 8, 128)
        y_s[c] = o["y_s"]
        k_s[0, c] = o["k_s"].reshape(64, 8, 128); v_s[0, c] = o["v_s"].reshape(64, 8, 128)
        c_s[0, c] = o["conv_sT"].reshape(128, 8, 2).transpose(2, 1, 0).reshape(2, 1024)
        if r == 3:
            c_p[0, b] = o["conv_pT"].reshape(128, 8, 2).transpose(2, 1, 0).reshape(2, 1024)
    return (y_p, y_s, k_p, v_p, c_p, k_s, v_s, c_s)
```
